# Optimizing a Trainium2 kernel written in Bass

```python
import jax, jax.numpy as jnp
from jax import lax
import numpy as np

D_MODEL = 1024
BATCH = 16
SEQ = 256
DEPTH = 1
DEC_BATCH = 2
DEC_SEQ = 2048
PAST_LEN = 256

GRID_W = 64
POS_THETA = 10000.0
EPS = 1e-6
N_MOD = 6
A_GROUPS = 4
A_GROUP_DIM = 128
A_WIDTH = A_GROUPS * A_GROUP_DIM
CHUNK_MLP = 128
H_B = 4
DK = 128
DV = 128
QK_WIDTH = H_B * DK
V_WIDTH = H_B * DV
CHUNK_REC = 32
IN_SPLITS = (A_WIDTH, A_WIDTH, QK_WIDTH, QK_WIDTH, QK_WIDTH, V_WIDTH, V_WIDTH, D_MODEL, D_MODEL)
IN_WIDTH = A_WIDTH * 2 + QK_WIDTH * 3 + V_WIDTH * 2 + D_MODEL * 2
PEER_HEADS = 8
N_KEYS = 128
N_EXPERTS = N_KEYS * N_KEYS
PEER_TOPK = 16
PEER_QHALF = 128
PEER_QDIM = 2 * PEER_QHALF
PEER_BLOCK = 128

kernel_name = "hybrid_gmlp_hgrn2_peer_diffusion_step"


def rmsnorm(x, g):
    xf = x.astype(jnp.float32)
    y = xf * lax.rsqrt(jnp.mean(xf * xf, axis=-1, keepdims=True) + EPS)
    return (y * g.astype(jnp.float32)).astype(x.dtype)


def grid_pos_embed(n_tokens):
    rows = n_tokens // GRID_W
    r = jnp.repeat(jnp.arange(rows, dtype=jnp.float32), GRID_W)
    col = jnp.tile(jnp.arange(GRID_W, dtype=jnp.float32), rows)
    quarter = D_MODEL // 4
    omega = 1.0 / (POS_THETA ** (jnp.arange(quarter, dtype=jnp.float32) / quarter))
    ar = r[:, None] * omega
    ac = col[:, None] * omega
    return jnp.concatenate([jnp.sin(ar), jnp.cos(ar), jnp.sin(ac), jnp.cos(ac)], axis=-1)


def adaln(cond, w_ada, b_ada):
    m = jax.nn.silu(cond) @ w_ada + b_ada
    return m.reshape(-1, N_MOD, D_MODEL)


def chunk_gmlp(zu, zv, norm_g, w_s, b_s):
    B, T, _ = zu.shape
    n = T // CHUNK_MLP
    u = jax.nn.gelu(zu)
    v = rmsnorm(jax.nn.gelu(zv), norm_g)
    vc = v.reshape(B, n, CHUNK_MLP, A_GROUPS, A_GROUP_DIM)
    vs = jnp.einsum('gts,bnsgc->bntgc', w_s, vc) + b_s.T[None, None, :, :, None]
    return u * vs.reshape(B, T, A_WIDTH)


def hgrn2_chunk_scan(q, k, v, logf, s0):
    B, H, T, _ = q.shape
    n = T // CHUNK_REC

    def to_chunks(a):
        return a.reshape(B, H, n, CHUNK_REC, a.shape[-1]).transpose(2, 0, 1, 3, 4)

    mask = jnp.tril(jnp.ones((CHUNK_REC, CHUNK_REC), dtype=bool))[:, :, None]

    def step(S, inp):
        qc, kc, vc, lfc = inp
        b = jnp.cumsum(lfc, axis=-2)
        diff = b[..., :, None, :] - b[..., None, :, :]
        decay = jnp.exp(jnp.where(mask, diff, -jnp.inf))
        scores = jnp.einsum('bhtk,bhsk,bhtsk->bhts', qc, kc, decay)
        o = (jnp.einsum('bhts,bhsv->bhtv', scores, vc)
             + jnp.einsum('bhtk,bhkv->bhtv', qc * jnp.exp(b), S))
        b_last = b[..., -1:, :]
        S_new = (jnp.exp(b_last[..., 0, :])[..., None] * S
                 + jnp.einsum('bhsk,bhsv->bhkv', kc * jnp.exp(b_last - b), vc))
        return S_new, o

    s_fin, o = lax.scan(step, s0, (to_chunks(q), to_chunks(k), to_chunks(v), to_chunks(logf)))
    o = o.transpose(1, 2, 0, 3, 4).reshape(B, H, T, v.shape[-1])
    return o, s_fin


def hgrn2_bidir(zq, zf_fw, zf_bw, zi, zg, lb, norm_g, s0):
    B, T, _ = zq.shape

    def heads(a):
        return a.reshape(B, T, H_B, -1).transpose(0, 2, 1, 3).astype(jnp.float32)

    def gates(zf, lb_d):
        f = lb_d + (1.0 - lb_d) * jax.nn.sigmoid(zf.astype(jnp.float32))
        return heads(1.0 - f), heads(jnp.log(f))

    q = heads(zq)
    v = heads(zi)
    k_fw, lf_fw = gates(zf_fw, lb[0])
    k_bw, lf_bw = gates(zf_bw, lb[1])
    o_fw, s_fw = hgrn2_chunk_scan(q, k_fw, v, lf_fw, s0[:, 0])
    flip = lambda a: jnp.flip(a, axis=2)
    o_bw, s_bw = hgrn2_chunk_scan(flip(q), flip(k_bw), flip(v), flip(lf_bw), s0[:, 1])
    o = (o_fw + flip(o_bw)).transpose(0, 2, 1, 3)
    o = rmsnorm(o, norm_g) * jax.nn.silu(zg.reshape(B, T, H_B, DV).astype(jnp.float32))
    return o.reshape(B, T, V_WIDTH).astype(zq.dtype), jnp.stack([s_fw, s_bw], axis=1)


def peer(x, w_q, sub_keys, u_tab, v_tab):
    shp = x.shape
    xt = x.reshape(-1, D_MODEL)
    n_tok = xt.shape[0]
    q = (xt @ w_q).reshape(n_tok, PEER_HEADS, 2, PEER_QHALF)
    s = jnp.einsum('thpc,hpkc->thpk', q, sub_keys).astype(jnp.float32)
    v1, i1 = lax.top_k(s[:, :, 0], PEER_TOPK)
    v2, i2 = lax.top_k(s[:, :, 1], PEER_TOPK)
    cand = (v1[..., :, None] + v2[..., None, :]).reshape(n_tok, PEER_HEADS, PEER_TOPK * PEER_TOPK)
    cand_idx = (i1[..., :, None] * N_KEYS + i2[..., None, :]).reshape(n_tok, PEER_HEADS, PEER_TOPK * PEER_TOPK)
    top_v, top_j = lax.top_k(cand, PEER_TOPK)
    idx = jnp.take_along_axis(cand_idx, top_j, axis=-1).reshape(n_tok, PEER_HEADS * PEER_TOPK)
    g = jax.nn.softmax(top_v, axis=-1).reshape(n_tok, PEER_HEADS * PEER_TOPK).astype(x.dtype)
    nb = n_tok // PEER_BLOCK

    def block(args):
        xb, ib, gb = args
        ue = jnp.take(u_tab, ib, axis=0)
        ve = jnp.take(v_tab, ib, axis=0)
        a = jax.nn.gelu(jnp.einsum('bd,bed->be', xb, ue)) * gb
        return jnp.einsum('be,bed->bd', a, ve)

    out = lax.map(block, (xt.reshape(nb, PEER_BLOCK, D_MODEL),
                          idx.reshape(nb, PEER_BLOCK, -1),
                          g.reshape(nb, PEER_BLOCK, -1)))
    return out.reshape(shp)


def trunk_layer(x, mod, s0, lb, norm1_g, w_in, sgu_norm_g, w_spatial, b_spatial, hgrn_norm_g,
                w_proj_a, w_proj_b, w_out, norm2_g, peer_w_q, peer_sub_keys, peer_u, peer_v):
    shift1, scale1, gate1, shift2, scale2, gate2 = [mod[:, i][:, None, :] for i in range(N_MOD)]
    h = rmsnorm(x, norm1_g) * (1.0 + scale1) + shift1
    z = h @ w_in
    offsets = []
    acc = 0
    for s in IN_SPLITS[:-1]:
        acc += s
        offsets.append(acc)
    zu, zv, zq, zf_fw, zf_bw, zi, zg, za, zb = jnp.split(z, offsets, axis=-1)
    y_a = chunk_gmlp(zu, zv, sgu_norm_g, w_spatial, b_spatial)
    y_b, s_fin = hgrn2_bidir(zq, zf_fw, zf_bw, zi, zg, lb, hgrn_norm_g, s0)
    mix = (jax.nn.sigmoid(za) * (y_a @ w_proj_a) + jax.nn.sigmoid(zb) * (y_b @ w_proj_b)) @ w_out
    x = x + gate1 * mix
    h = rmsnorm(x, norm2_g) * (1.0 + scale2) + shift2
    x = x + gate2 * peer(h, peer_w_q, peer_sub_keys, peer_u, peer_v)
    return x, s_fin


def setup_inputs(seed: int = 0) -> dict:
    key = jax.random.key(seed)
    ks = jax.random.split(key, 24)
    nrm = lambda k, shape, scale: jax.random.normal(k, shape, jnp.float32) * scale
    Dinv = D_MODEL ** -0.5
    return {
        "x_prompt": nrm(ks[0], (BATCH, SEQ, D_MODEL), 1.0),
        "x_sample": nrm(ks[1], (DEC_BATCH, DEC_SEQ, D_MODEL), 1.0),
        "state_hgrn": nrm(ks[2], (DEC_BATCH, DEPTH, 2, H_B, DK, DV), 0.5),
        "c": nrm(ks[3], (DEC_BATCH, D_MODEL), 1.0),
        "c_ctx": nrm(ks[4], (D_MODEL,), 1.0),
        "w_ada": nrm(ks[5], (DEPTH, D_MODEL, N_MOD * D_MODEL), Dinv),
        "b_ada": nrm(ks[6], (DEPTH, N_MOD * D_MODEL), 0.02),
        "norm1_g": 1.0 + nrm(ks[7], (DEPTH, D_MODEL), 0.02),
        "w_in": nrm(ks[8], (DEPTH, D_MODEL, IN_WIDTH), Dinv),
        "sgu_norm_g": 1.0 + nrm(ks[9], (DEPTH, A_WIDTH), 0.02),
        "w_spatial": nrm(ks[10], (DEPTH, A_GROUPS, CHUNK_MLP, CHUNK_MLP), CHUNK_MLP ** -0.5),
        "b_spatial": 1.0 + nrm(ks[11], (DEPTH, A_GROUPS, CHUNK_MLP), 0.02),
        "hgrn_lb": nrm(ks[12], (DEPTH + 1, 2, QK_WIDTH), 0.5),
        "hgrn_norm_g": 1.0 + nrm(ks[13], (DEPTH, H_B, DV), 0.02),
        "w_proj_a": nrm(ks[14], (DEPTH, A_WIDTH, D_MODEL), A_WIDTH ** -0.5),
        "w_proj_b": nrm(ks[15], (DEPTH, V_WIDTH, D_MODEL), V_WIDTH ** -0.5),
        "w_out": nrm(ks[16], (DEPTH, D_MODEL, D_MODEL), Dinv),
        "norm2_g": 1.0 + nrm(ks[17], (DEPTH, D_MODEL), 0.02),
        "peer_w_q": nrm(ks[18], (DEPTH, D_MODEL, PEER_HEADS * PEER_QDIM), Dinv),
        "peer_sub_keys": nrm(ks[19], (DEPTH, PEER_HEADS, 2, N_KEYS, PEER_QHALF), PEER_QHALF ** -0.5),
        "peer_u": nrm(ks[20], (DEPTH, N_EXPERTS, D_MODEL), Dinv),
        "peer_v": nrm(ks[21], (DEPTH, N_EXPERTS, D_MODEL), (PEER_HEADS * PEER_TOPK) ** -0.5),
        "final_norm_g": 1.0 + nrm(ks[22], (D_MODEL,), 0.02),
    }


def reference(x_prompt, x_sample, state_hgrn, c, c_ctx, w_ada, b_ada, norm1_g, w_in, sgu_norm_g,
              w_spatial, b_spatial, hgrn_lb, hgrn_norm_g, w_proj_a, w_proj_b, w_out, norm2_g,
              peer_w_q, peer_sub_keys, peer_u, peer_v, final_norm_g):
    lb = jnp.cumsum(jax.nn.softmax(hgrn_lb.astype(jnp.float32), axis=0), axis=0)[:DEPTH]
    xp = x_prompt
    xs = x_sample + grid_pos_embed(x_sample.shape[1]).astype(x_sample.dtype)[None]
    s_ctx0 = jnp.zeros((x_prompt.shape[0], 2, H_B, DK, DV), jnp.float32)
    ctx_states = []
    for l in range(DEPTH):
        params = (norm1_g[l], w_in[l], sgu_norm_g[l], w_spatial[l], b_spatial[l], hgrn_norm_g[l],
                  w_proj_a[l], w_proj_b[l], w_out[l], norm2_g[l], peer_w_q[l], peer_sub_keys[l],
                  peer_u[l], peer_v[l])
        mod_ctx = adaln(c_ctx[None].astype(xp.dtype), w_ada[l], b_ada[l])
        mod_lat = adaln(c, w_ada[l], b_ada[l])
        xp, s_ctx = trunk_layer(xp, mod_ctx, s_ctx0, lb[l], *params)
        xs, _ = trunk_layer(xs, mod_lat, state_hgrn[:, l].astype(jnp.float32), lb[l], *params)
        ctx_states.append(s_ctx)
    y_prompt = rmsnorm(xp, final_norm_g)
    y_sample = rmsnorm(xs, final_norm_g)
    new_state_hgrn = jnp.stack(ctx_states, axis=1)
    return (y_prompt, y_sample, new_state_hgrn)
```

```python
import math
import types
from contextlib import ExitStack
import numpy as np
import concourse.bass as bass
import concourse.mybir as mybir
from concourse.bass_utils import run_bass_kernel_spmd

F32 = mybir.dt.float32
BF16 = mybir.dt.bfloat16
I32 = mybir.dt.int32
U32 = mybir.dt.uint32
AF = mybir.ActivationFunctionType
ALU = mybir.AluOpType
AX = mybir.AxisListType

ENGS = ["tensor", "vector", "scalar", "gpsimd", "sync"]
N_DMA_SLOTS = 14
EPS = 1e-6
NEG = -1.0e30


def _snap(fn):
    if fn.__closure__ is None:
        return fn
    cells = tuple(types.CellType(c.cell_contents) for c in fn.__closure__)
    return types.FunctionType(fn.__code__, fn.__globals__, fn.__name__, fn.__defaults__, cells)


class Sched:
    def __init__(self, nc, stack):
        self.nc = nc
        self.prog = {e: [] for e in ENGS}
        self.count = {e: 0 for e in ENGS}
        self.sem = {e: stack.enter_context(nc.semaphore("s_" + e)) for e in ENGS if e != "sync"}
        self.dsem = [stack.enter_context(nc.semaphore("d_%d" % i)) for i in range(N_DMA_SLOTS)]
        self.dcount = [0] * N_DMA_SLOTS
        self.dnext = 0
        self.seen = {e: {} for e in ENGS}
        self.snap = {}
        self.bufs = {}
        self.pool_out = []

    def _deps(self, reads, writes):
        deps = []
        for k in reads:
            b = self.bufs.get(k)
            if b and b["w"] is not None:
                deps.append(b["w"])
        for k in writes:
            b = self.bufs.get(k)
            if b:
                if b["w"] is not None:
                    deps.append(b["w"])
                deps.extend(b["r"])
        return deps

    def _record(self, tok, reads, writes):
        for k in reads:
            b = self.bufs.setdefault(k, {"w": None, "r": []})
            b["r"].append(tok)
        for k in writes:
            self.bufs[k] = {"w": tok, "r": []}

    def _emit_waits(self, eng, deps):
        seen = self.seen[eng]
        need = {}
        for key, val in deps:
            if key == eng and eng == "tensor":
                continue
            if seen.get(key, 0) >= val:
                continue
            if need.get(key, 0) < val:
                need[key] = val
        for key, val in sorted(need.items(), key=lambda kv: -kv[1]):
            if seen.get(key, 0) >= val:
                continue
            sem = self.dsem[key[1]] if isinstance(key, tuple) else self.sem[key]
            self.prog[eng].append(("wait", sem, val))
            seen[key] = val
            sn = self.snap.get((key, val))
            if sn:
                for k2, v2 in sn.items():
                    if seen.get(k2, 0) < v2:
                        seen[k2] = v2

    def op(self, eng, fn, reads=(), writes=()):
        self._emit_waits(eng, self._deps(reads, writes))
        self.count[eng] += 1
        idx = self.count[eng]
        self.prog[eng].append(("op", _snap(fn)))
        tok = (eng, idx)
        self.snap[tok] = dict(self.seen[eng])
        self._record(tok, reads, writes)
        return tok

    def dma(self, eng, out, in_, reads=(), writes=(), **kw):
        return self.dma_fn(eng, lambda e: e.dma_start(out=out, in_=in_, **kw), reads, writes)

    def dma_fn(self, eng, fn, reads=(), writes=()):
        deps = self._deps(reads, writes)
        slot = self.dnext
        self.dnext = (self.dnext + 1) % N_DMA_SLOTS
        key = ("d", slot)
        if self.dcount[slot] > 0:
            deps.append((key, self.dcount[slot]))
        if eng == "gpsimd":
            if len(self.pool_out) >= 6:
                deps.append(self.pool_out.pop(0))
        self._emit_waits(eng, deps)
        self.dcount[slot] += 16
        val = self.dcount[slot]
        self.prog[eng].append(("dma", _snap(fn), self.dsem[slot]))
        tok = (key, val)
        if eng == "gpsimd":
            self.pool_out.append(tok)
        self.snap[tok] = dict(self.seen[eng])
        self._record(tok, reads, writes)
        return tok

    def _all_tokens(self):
        deps = [(e, self.count[e]) for e in ENGS if e != "sync" and self.count[e] > 0]
        deps += [(("d", s), self.dcount[s]) for s in range(N_DMA_SLOTS) if self.dcount[s] > 0]
        return deps

    def barrier(self):
        deps = self._all_tokens()
        for e in ENGS:
            self._emit_waits(e, deps)
        self.bufs = {}

    def finish(self):
        self._emit_waits("sync", self._all_tokens())

    def emit(self):
        nc = self.nc
        with nc.Block() as block:
            def runner(name):
                def _run(e):
                    for item in self.prog[name]:
                        if item[0] == "wait":
                            e.wait_ge(item[1], item[2])
                        elif item[0] == "op":
                            item[1](e).then_inc(self.sem[name], 1)
                        else:
                            item[1](e).then_inc(item[2], 16)
                return _run
            block.tensor(runner("tensor"))
            block.vector(runner("vector"))
            block.scalar(runner("scalar"))
            block.gpsimd(runner("gpsimd"))
            block.sync(runner("sync"))


CST_LAYOUT = {}


def make_consts():
    r = np.arange(128)
    same = (r[:, None] // 32) == (r[None, :] // 32)
    parts = [
        ("ident", np.eye(128)),
        ("mfw", same & (r[:, None] <= r[None, :])),
        ("mbw", same & (r[:, None] >= r[None, :])),
        ("m2fw", same & (r[:, None] > r[None, :])),
        ("m2bw", same & (r[:, None] < r[None, :])),
        ("tris", r[:, None] > r[None, :]),
        ("ones", np.ones((128, 128))),
        ("ci", (r[:, None] // 32) == np.arange(4)[None, :]),
        ("iota16", np.broadcast_to(np.arange(16)[None, :], (128, 16))),
        ("freqidx", np.broadcast_to(np.arange(256)[None, :], (128, 256))),
        ("pcol", r[:, None]),
        ("iota128", np.broadcast_to(np.arange(128)[None, :], (128, 128))),
    ]
    off = 0
    cols = []
    for name, a in parts:
        a = np.asarray(a, dtype=np.float32)
        CST_LAYOUT[name] = (off, a.shape[1])
        off += a.shape[1]
        cols.append(a)
    return np.ascontiguousarray(np.concatenate(cols, axis=1))


_CST = make_consts()
NCST = _CST.shape[1]

DEBUG_OUTS = {}


def build_program(debug=False):
    nc = bass.Bass("TRN2", target_bir_lowering=False)

    def din(name, shape, dt=F32):
        return nc.dram_tensor(name, list(shape), dt, kind="ExternalInput").ap()

    def dout(name, shape, dt=F32):
        return nc.dram_tensor(name, list(shape), dt, kind="ExternalOutput").ap()

    xp_d = din("xp", [512, 1024]); xs_d = din("xs", [512, 1024]); xo_d = din("xo", [3, 512, 1024])
    sel_d = din("sel", [16, 2, 128, 128])
    s0_d = din("s0", [2, 4, 128, 128]); condT_d = din("condT", [128, 16]); alpha_d = din("alpha", [128, 3])
    wfs_d = din("wfs", [3, 1024, 512]); lbs_d = din("lbs", [1, 3 * 2 * 512])
    cst_d = din("cst", [128, NCST])
    w_ada_d = din("w_ada", [1024, 6144]); b_adaT_d = din("b_adaT", [128, 48])
    n1T_d = din("n1T", [128, 8]); n2T_d = din("n2T", [128, 8])
    w_in_d = din("w_in", [1024, 5632]); sguT_d = din("sguT", [128, 4]); w_sp_d = din("w_sp", [4, 128, 128])
    b_sp_d = din("b_sp", [1, 512]); lb_d = din("lb", [1, 2048]); hgT_d = din("hgT", [128, 4])
    wpa_d = din("wpa", [512, 1024]); wpb_d = din("wpb", [512, 1024]); wo_d = din("wo", [1024, 1024])
    wq_d = din("wq", [1024, 2048]); sk_d = din("sk", [16, 128, 128])
    pu_d = din("pu", [128, 128, 1024]); pv_d = din("pv", [16384, 1024]); fg_d = din("fg", [1, 1024])

    yp_d = dout("yp", [512, 1024]); ys_d = dout("ys", [512, 1024]); st_d = dout("st", [2, 2, 4, 128, 128])
    xres_d = nc.dram_tensor("xres", [1024, 1024], F32, kind="Internal").ap()
    x1s_d = nc.dram_tensor("x1s", [1024, 1024], F32, kind="Internal").ap()
    h2s_d = nc.dram_tensor("h2s", [1024, 1024], BF16, kind="Internal").ap()
    gd_d = nc.dram_tensor("gd", [128, 128, 1024], BF16, kind="Internal").ap()
    gls_d = nc.dram_tensor("gls", [128, 128, 1024], BF16, kind="Internal").ap()

    dbg_list = []

    with ExitStack() as gst:
        S = Sched(nc, gst)

        def mk(stack):
            def T(name, shape, dt=F32):
                return stack.enter_context(nc.sbuf_tensor("sb_" + name, list(shape), dt))
            return T
        GT = mk(gst)

        PS = {n: gst.enter_context(nc.psum_tensor("ps_" + n, [128, 512], F32)) for n in "abcdefg"}
        PT = gst.enter_context(nc.psum_tensor("ps_t", [128, 1024], BF16))

        def dbg(name, ap, shape, reads, dt=F32):
            if not debug:
                return
            d = dout("dbg_" + name, shape, dt)
            DEBUG_OUTS[name] = (tuple(shape), dt)
            S.dma("sync", d, ap, reads=reads)

        _breg = {}

        def breg(e):
            if "r" not in _breg:
                _breg["r"] = e.to_reg(16383)
            return _breg["r"]

        V = lambda fn, r=(), w=(): S.op("vector", fn, r, w)
        A = lambda fn, r=(), w=(): S.op("scalar", fn, r, w)
        G = lambda fn, r=(), w=(): S.op("gpsimd", fn, r, w)
        P = lambda fn, r=(), w=(): S.op("tensor", fn, r, w)

        cst = GT("cst", [128, NCST])
        S.dma("sync", cst[:], cst_d, writes=["cst"])

        def C(name):
            o, n = CST_LAYOUT[name]
            return cst[:, o:o + n]

        ident_bf = GT("ident_bf", [128, 128], BF16)
        ones_bf = GT("ones_bf", [128, 128], BF16)
        V(lambda e: e.tensor_copy(out=ident_bf[:], in_=C("ident")), ["cst"], ["ident_bf"])
        V(lambda e: e.tensor_copy(out=ones_bf[:], in_=C("ones")), ["cst"], ["ones_bf"])

        modT = GT("modT", [128, 48, 2])
        sc = GT("sc", [128, 8, 2], BF16); b_adaT = GT("b_adaT", [128, 48])
        G1T = GT("G1T", [128, 8, 2]); G2T = GT("G2T", [128, 8, 2])
        n1T = GT("n1T", [128, 8]); n2T = GT("n2T", [128, 8])
        sguT = GT("sguT", [128, 4]); hgT = GT("hgT", [128, 4])
        bsb = GT("bsb", [128, 512])
        wsT = GT("wsT", [128, 4, 128], BF16)
        dg = GT("dg", [128, 8, 128])
        alpha = GT("alpha", [128, 3]); oma = GT("oma", [128, 3])
        junkb = GT("junkb", [128, 1024], BF16)
        stat = GT("stat", [128, 8])
        tmpf = GT("tmpf", [128, 1024])
        h2T = GT("h2T", [128, 8, 8, 128], BF16)
        L1 = gst.enter_context(ExitStack()); T_L1 = mk(L1)
        hT_own = T_L1("hT_own", [128, 8, 8, 128], BF16)
        ybT = T_L1("ybT", [128, 8, 4, 128], BF16)
        L2 = L1.enter_context(ExitStack()); T_L2 = mk(L2)
        lb_b = T_L2("lb_b", [128, 2, 512]); omlb_b = T_L2("omlb_b", [128, 2, 512])
        G1b = T_L2("G1b", [128, 2, 1024], BF16); sh1b = T_L2("sh1b", [128, 2, 1024], BF16)
        Sst = T_L2("Sst", [128, 3, 2, 4, 128])
        tab = T_L2("tab", [128, 512])
        xt = [T_L2("xt%d" % i, [128, 1024]) for i in range(2)]
        hb = T_L2("hb", [128, 1024], BF16)

        for (dst, src, nm) in [(n1T, n1T_d, "n1T"), (n2T, n2T_d, "n2T"), (sguT, sguT_d, "sguT"), (hgT, hgT_d, "hgT"),
                               (alpha, alpha_d, "alpha")]:
            S.dma("sync", dst[:], src, writes=[nm])
        S.dma("sync", bsb[:], b_sp_d.partition_broadcast(128), writes=["bsb"])
        V(lambda e: e.tensor_scalar(out=oma[:], in0=alpha[:], scalar1=-1.0, scalar2=1.0, op0=ALU.mult, op1=ALU.add), ["alpha"], ["oma"])
        for sq in range(2):
            G(lambda e, sq=sq: e.memset(Sst[:, sq, :, :, :], 0.0), [], ["S%d0" % sq, "S%d1" % sq])
        for d in range(2):
            S.dma("sync", Sst[:, 2, d, :, :], s0_d[d].rearrange("h k v -> k h v"), writes=["S2%d" % d])

        def bcast_tile(dst_ap, val_ap, val_keys, dst_key):
            V(lambda e: e.tensor_tensor(out=dg[:], in0=C("ident").unsqueeze(1).to_broadcast([128, 8, 128]),
                                        in1=val_ap.unsqueeze(2).to_broadcast([128, 8, 128]), op=ALU.mult), ["cst"] + val_keys, ["dg"])
            for k in range(8):
                bank = "b" if k < 4 else "c"
                P(lambda e, k=k, bank=bank: e.matmul(PS[bank][:, (k % 4) * 128:(k % 4 + 1) * 128], lhsT=C("ones"), rhs=dg[:, k, :],
                                                     start=True, stop=True), ["cst", "dg"], ["p" + bank])
            A(lambda e: e.activation(out=dst_ap[:, 0:512], in_=PS["b"][:], func=AF.Copy), ["pb"], [dst_key + "lo"])
            A(lambda e: e.activation(out=dst_ap[:, 512:1024], in_=PS["c"][:], func=AF.Copy), ["pc"], [dst_key + "hi"])

        st1 = ExitStack()
        T1 = mk(st1)
        wfs = T1("wfs", [128, 3, 8, 512], BF16)
        wi = T1("wi", [128, 8, 512], BF16)
        w_in_v = w_in_d.rearrange("(k p) c -> p k c", p=128)
        with ExitStack() as st0:
            T0 = mk(st0)
            condT = T0("condT", [128, 16])
            wa = [T0("wa%d" % i, [128, 8, 512], BF16) for i in range(2)]
            lbr = T0("lbr", [128, 2048])
            wsp = T0("wsp", [128, 4, 128])
            t512 = [T0("t512_%d" % i, [128, 512]) for i in range(3)]
            ti512 = T0("ti512", [128, 512], I32)

            S.dma("sync", condT[:], condT_d, writes=["condT"])
            S.dma("sync", b_adaT[:], b_adaT_d, writes=["b_adaT"])
            S.dma("sync", lbr[:], lb_d.partition_broadcast(128), writes=["lbr"])
            S.dma("sync", wsp[:], w_sp_d.rearrange("g t s -> t g s"), writes=["wsp"])

            A(lambda e: e.activation(out=sc[:].rearrange("p k j -> p (k j)"), in_=condT[:], func=AF.Silu), ["condT"], ["sc"])
            w_ada_v = w_ada_d.rearrange("(k p) c -> p k c", p=128)
            for blk in range(4):
                wb = wa[blk % 2]
                S.dma("gpsimd", wb[:], w_ada_v[:, :, blk * 512:(blk + 1) * 512], writes=["wa%d" % (blk % 2)])
                for cc in range(4):
                    col = (blk * 4 + cc) * 2
                    for k in range(8):
                        P(lambda e, wb=wb, cc=cc, k=k, col=col: e.matmul(PS["a"][:, col:col + 2], lhsT=wb[:, k, cc * 128:(cc + 1) * 128],
                                                                         rhs=sc[:, k, :], start=(k == 0), stop=(k == 7)),
                          ["wa%d" % (blk % 2), "sc"], ["pa"])
            for s_ in range(3):
                S.dma("gpsimd", wfs[:, s_, :, :], wfs_d[s_].rearrange("(k p) c -> p k c", p=128), writes=["wfs"])
            S.dma("gpsimd", wi[:], w_in_v[:, :, 2560:3072], writes=["wi"])
            V(lambda e: e.tensor_tensor(out=modT[:, 0:16, :], in0=PS["a"][:, 0:32].rearrange("p (c j) -> p c j", j=2),
                                        in1=b_adaT[:, 0:16].unsqueeze(2).to_broadcast([128, 16, 2]), op=ALU.add), ["pa", "b_adaT"], ["modT"])
            for (GT_, nT, m, nm, nnm) in [(G1T, n1T, 1, "G1T", "n1T")]:
                V(lambda e, GT_=GT_, nT=nT, m=m: e.scalar_tensor_tensor(out=GT_[:], in0=modT[:, m * 8:(m + 1) * 8, :], scalar=1.0,
                                                                      in1=nT[:].unsqueeze(2).to_broadcast([128, 8, 2]), op0=ALU.add, op1=ALU.mult),
                  ["modT", nnm], [nm])

            for j in range(2):
                bcast_tile(G1b[:, j, :], G1T[:, :, j], ["G1T"], "G1b%d" % j)
                bcast_tile(sh1b[:, j, :], modT[:, 0:8, j], ["modT"], "sh1b%d" % j)

            V(lambda e: e.tensor_tensor(out=lb_b[:].rearrange("p d c -> p (d c)"), in0=lbr[:, 0:1024], in1=lbr[:, 1024:2048], op=ALU.subtract),
              ["lbr"], ["lb_b"])
            A(lambda e: e.activation(out=lb_b[:], in_=lb_b[:], func=AF.Sigmoid), ["lb_b"], ["lb_b"])
            V(lambda e: e.tensor_scalar(out=omlb_b[:], in0=lb_b[:], scalar1=-1.0, scalar2=1.0, op0=ALU.mult, op1=ALU.add), ["lb_b"], ["omlb_b"])

            for g in range(4):
                P(lambda e, g=g: e.transpose(out=PS["d"][:, g * 128:(g + 1) * 128], in_=wsp[:, g, :], identity=C("ident")), ["wsp", "cst"], ["pd"])
            A(lambda e: e.activation(out=wsT[:].rearrange("p g t -> p (g t)"), in_=PS["d"][:], func=AF.Copy), ["pd"], ["wsT"])
            om = t512[0]
            A(lambda e: e.activation(out=om[:, 0:256], in_=C("freqidx"), func=AF.Exp, scale=-math.log(10000.0) / 256.0), ["cst"], ["om"])
            arg = t512[1]
            V(lambda e: e.tensor_scalar(out=arg[:, 0:256], in0=om[:, 0:256], scalar1=C("pcol"), scalar2=None, op0=ALU.mult), ["om", "cst"], ["arg"])
            V(lambda e: e.tensor_scalar(out=arg[:, 256:512], in0=arg[:, 0:256], scalar1=math.pi / 2, scalar2=None, op0=ALU.add), ["arg"], ["arg"])
            tt = t512[2]
            V(lambda e: e.tensor_scalar(out=tt[:], in0=arg[:], scalar1=1.0 / (2 * math.pi), scalar2=0.5, op0=ALU.mult, op1=ALU.add), ["arg"], ["tt"])
            V(lambda e: e.tensor_copy(out=ti512[:], in_=tt[:]), ["tt"], ["ti"])
            V(lambda e: e.tensor_copy(out=om[:], in_=ti512[:]), ["ti"], ["om"])
            V(lambda e: e.tensor_tensor(out=tt[:], in0=tt[:], in1=om[:], op=ALU.subtract), ["tt", "om"], ["tt"])
            V(lambda e: e.tensor_scalar(out=om[:], in0=tt[:], scalar1=0.0, scalar2=None, op0=ALU.is_lt), ["tt"], ["om"])
            V(lambda e: e.tensor_tensor(out=tt[:], in0=tt[:], in1=om[:], op=ALU.add), ["tt", "om"], ["tt"])
            V(lambda e: e.tensor_scalar(out=tt[:], in0=tt[:], scalar1=2 * math.pi, scalar2=-math.pi, op0=ALU.mult, op1=ALU.add), ["tt"], ["tt"])
            V(lambda e: e.tensor_scalar(out=tt[:], in0=tt[:], scalar1=3.14159, scalar2=-3.14159, op0=ALU.min, op1=ALU.max), ["tt"], ["tt"])
            A(lambda e: e.activation(out=tab[:], in_=tt[:], func=AF.Sin), ["tt"], ["tab"])
            S.barrier()

        fe_cnt = [0]

        def front(x_src, j, sel_idx, hT_dst, hT_key, res_row=None, selbuf=None):
            i = fe_cnt[0] % 2
            fe_cnt[0] += 1
            x = xt[i]
            xk = "xt%d" % i
            S.dma("sync", x[:], x_src, writes=[xk])
            if sel_idx is not None:
                sb = selbuf[i]
                S.dma("sync", sb[:], sel_d[sel_idx].rearrange("a p n -> p a n"), writes=["selb%d" % i])
                P(lambda e: e.matmul(PS["a"][:], lhsT=sb[:, 0, :], rhs=tab[:], start=True, stop=True), ["selb%d" % i, "tab"], ["pa"])
                P(lambda e: e.matmul(PS["b"][:], lhsT=sb[:, 1, :], rhs=tab[:], start=True, stop=True), ["selb%d" % i, "tab"], ["pb"])
                V(lambda e: e.tensor_tensor(out=x[:, 0:512], in0=x[:, 0:512], in1=PS["a"][:], op=ALU.add), [xk, "pa"], [xk])
                V(lambda e: e.tensor_tensor(out=x[:, 512:1024], in0=x[:, 512:1024], in1=PS["b"][:], op=ALU.add), [xk, "pb"], [xk])
            if res_row is not None:
                S.dma("sync", xres_d[res_row:res_row + 128, :], x[:], reads=[xk], writes=["xres%d" % res_row])
            A(lambda e: e.activation(out=junkb[:], in_=x[:], func=AF.Square, accum_out=stat[:, 0:1]), [xk], ["junkb", "stat0"])
            A(lambda e: e.activation(out=stat[:, 1:2], in_=stat[:, 0:1], func=AF.Ln, scale=1.0 / 1024.0, bias=EPS), ["stat0"], ["stat1"])
            A(lambda e: e.activation(out=stat[:, 2:3], in_=stat[:, 1:2], func=AF.Exp, scale=-0.5), ["stat1"], ["stat2"])
            V(lambda e: e.scalar_tensor_tensor(out=tmpf[:], in0=x[:], scalar=stat[:, 2:3], in1=G1b[:, j, :], op0=ALU.mult, op1=ALU.mult),
              [xk, "stat2"], ["tmpf"])
            V(lambda e: e.tensor_tensor(out=hb[:], in0=tmpf[:], in1=sh1b[:, j, :], op=ALU.add), ["tmpf"], ["hb"])
            for k in range(8):
                P(lambda e, k=k: e.transpose(out=PT[:, k * 128:(k + 1) * 128], in_=hb[:, k * 128:(k + 1) * 128], identity=ident_bf[:]),
                  ["hb", "ident_bf"], ["pt"])
            A(lambda e: e.activation(out=hT_dst.rearrange("p k t -> p (k t)"), in_=PT[:], func=AF.Copy), ["pt"], [hT_key])

        def proj_tm(ps_ap, ps_key, hT, hT_key, w, w_key, c0, ncols):
            for k in range(8):
                P(lambda e, k=k: e.matmul(ps_ap, lhsT=hT[:, k, :], rhs=w[:, k, c0:c0 + ncols], start=(k == 0), stop=(k == 7)),
                  [hT_key, w_key], [ps_key])

        def proj_fm(ps_ap, ps_key, hT, hT_key, w, w_key, c0):
            for k in range(8):
                P(lambda e, k=k: e.matmul(ps_ap, lhsT=w[:, k, c0:c0 + 128], rhs=hT[:, k, :], start=(k == 0), stop=(k == 7)),
                  [hT_key, w_key], [ps_key])

        w_in_v = w_in_d.rearrange("(k p) c -> p k c", p=128)

        with st1:
            selbuf = [T1("selb%d" % i, [128, 2, 128]) for i in range(2)]
            lbs_r = T1("lbs_r", [128, 3, 2, 512])
            lb_s = T1("lb_s", [128, 3, 512]); omlb_s = T1("omlb_s", [128, 3, 512])
            hTs = [T1("hTs%d" % i, [128, 8, 128], BF16) for i in range(2)]
            lf_s = T1("lf_s", [128, 4, 512]); kk_s = T1("kk_s", [128, 4, 512])
            v_s = T1("v_s", [128, 4, 512], BF16); kd_s = T1("kd_s", [128, 4, 512], BF16)
            fs = T1("fs", [128, 512]); eE = T1("eE", [128, 512])
            A_s = T1("A_s", [128, 4]); Am1 = T1("Am1", [128, 4]); Ab = T1("Ab", [128, 2, 4])
            tS = T1("tS", [128, 512])
            wa2 = [T1("wa2_%d" % i, [128, 8, 256], BF16) for i in range(2)]
            w_ada_v2 = w_ada_d.rearrange("(k p) c -> p k c", p=128)

            def ada_sub(sb):
                wb = wa2[sb % 2]
                c0 = 2048 + sb * 256
                S.dma("gpsimd", wb[:], w_ada_v2[:, :, c0:c0 + 256], writes=["wa2_%d" % (sb % 2)])
                for cc in range(2):
                    col = 64 + (sb * 2 + cc) * 2
                    for k in range(8):
                        P(lambda e: e.matmul(PS["f"][:, col:col + 2], lhsT=wb[:, k, cc * 128:(cc + 1) * 128], rhs=sc[:, k, :],
                                             start=(k == 0), stop=(k == 7)), ["wa2_%d" % (sb % 2), "sc"], ["pf2"])

            S.dma("sync", lbs_r[:].rearrange("p s l c -> p (s l c)"), lbs_d.partition_broadcast(128), writes=["lbs_r"])
            V(lambda e: e.tensor_tensor(out=lb_s[:], in0=lbs_r[:, :, 0, :], in1=lbs_r[:, :, 1, :], op=ALU.subtract), ["lbs_r"], ["lb_s"])
            A(lambda e: e.activation(out=lb_s[:], in_=lb_s[:], func=AF.Sigmoid), ["lb_s"], ["lb_s"])
            V(lambda e: e.tensor_scalar(out=omlb_s[:], in0=lb_s[:], scalar1=-1.0, scalar2=1.0, op0=ALU.mult, op1=ALU.add), ["lb_s"], ["omlb_s"])

            for t in range(8):
                if t < 4:
                    front(xp_d[t * 128:(t + 1) * 128, :], 0, None, hT_own[:, t, :, :], "hT%d" % t, res_row=t * 128)
                else:
                    front(xs_d[(t - 4) * 128:(t - 3) * 128, :], 1, t - 4, hT_own[:, t, :, :], "hT%d" % t, res_row=t * 128, selbuf=selbuf)
                if t % 2 == 1:
                    ada_sub(t // 2)
            if debug:
                dbg("hT0", hT_own[:, 0, :, :], [128, 8, 128], ["hT0"], BF16)
                dbg("hT4", hT_own[:, 4, :, :], [128, 8, 128], ["hT4"], BF16)

            stiles = [(s_, j_) for s_ in range(3) for j_ in range(4)]

            def slot_front(idx):
                s_, j_ = stiles[idx]
                front(xo_d[s_, j_ * 128:(j_ + 1) * 128, :], 1, 4 + s_ * 4 + j_, hTs[idx % 2][:], "hTs%d" % (idx % 2), selbuf=selbuf)

            slot_front(0)
            for sidx, (s, jj) in enumerate(stiles):
                if True:
                    if sidx + 1 < 12:
                        slot_front(sidx + 1)
                    ada_sub(4 + sidx)
                    hh = hTs[sidx % 2]
                    hk = "hTs%d" % (sidx % 2)
                    proj_tm(PS["c"][:], "pc", hh, hk, wfs[:, s, :, :], "wfs", 0, 512)
                    proj_tm(PS["d"][:], "pd", hh, hk, wi, "wi", 0, 512)
                    A(lambda e: e.activation(out=fs[:], in_=PS["c"][:], func=AF.Exp, scale=-1.0), ["pc"], ["fs"])
                    A(lambda e: e.activation(out=fs[:], in_=fs[:], func=AF.Ln, bias=1.0), ["fs"], ["fs"])
                    A(lambda e: e.activation(out=fs[:], in_=fs[:], func=AF.Exp, scale=-1.0), ["fs"], ["fs"])
                    A(lambda e, jj=jj: e.activation(out=v_s[:, jj, :], in_=PS["d"][:], func=AF.Copy), ["pd"], ["v_s%d" % jj])
                    V(lambda e, s=s: e.tensor_tensor(out=fs[:], in0=fs[:], in1=omlb_s[:, s, :], op=ALU.mult), ["fs", "omlb_s"], ["fs"])
                    V(lambda e, s=s: e.tensor_tensor(out=fs[:], in0=fs[:], in1=lb_s[:, s, :], op=ALU.add), ["fs", "lb_s"], ["fs"])
                    A(lambda e, jj=jj: e.activation(out=lf_s[:, jj, :], in_=fs[:], func=AF.Ln), ["fs"], ["lf_s%d" % jj])
                    V(lambda e, jj=jj: e.tensor_scalar(out=kk_s[:, jj, :], in0=fs[:], scalar1=-1.0, scalar2=1.0, op0=ALU.mult, op1=ALU.add),
                      ["fs"], ["kk_s%d" % jj])
                if jj != 3:
                    continue
                for jj in range(4):
                    P(lambda e, jj=jj: e.matmul(PS["e"][:], lhsT=C("tris"), rhs=lf_s[:, jj, :], start=True, stop=(jj == 3)),
                      ["cst", "lf_s%d" % jj], ["pe"])
                    for j2 in range(jj + 1, 4):
                        P(lambda e, j2=j2: e.matmul(PS["e"][:], lhsT=C("ones"), rhs=lf_s[:, j2, :], start=False, stop=(j2 == 3)),
                          ["cst", "lf_s%d" % j2], ["pe"])
                    A(lambda e: e.activation(out=eE[:], in_=PS["e"][:], func=AF.Exp), ["pe"], ["eE"])
                    V(lambda e, jj=jj: e.tensor_tensor(out=kd_s[:, jj, :], in0=kk_s[:, jj, :], in1=eE[:], op=ALU.mult),
                      ["eE", "kk_s%d" % jj], ["kd_s%d" % jj])
                for h in range(4):
                    for jj in range(4):
                        P(lambda e, h=h, jj=jj: e.matmul(PS["f"][:, 2 * h:2 * h + 2], lhsT=lf_s[:, jj, h * 128:(h + 1) * 128], rhs=C("ones")[:, 0:2],
                                                         start=(jj == 0), stop=(jj == 3)), ["cst", "lf_s%d" % jj], ["pf"])
                for h in range(4):
                    for jj in range(4):
                        P(lambda e, h=h, jj=jj: e.matmul(PS["g"][:, h * 128:(h + 1) * 128], lhsT=kd_s[:, jj, h * 128:(h + 1) * 128],
                                                         rhs=v_s[:, jj, h * 128:(h + 1) * 128], start=(jj == 0), stop=(jj == 3)),
                          ["kd_s%d" % jj, "v_s%d" % jj], ["pg"])
                A(lambda e: e.activation(out=A_s[:], in_=PS["f"][:, 0:8].rearrange("p (h two) -> p h two", two=2)[:, :, 0], func=AF.Exp), ["pf"], ["A_s"])
                V(lambda e: e.tensor_scalar(out=Am1[:], in0=A_s[:], scalar1=-1.0, scalar2=None, op0=ALU.add), ["A_s"], ["Am1"])
                for d, acol in enumerate([alpha, oma]):
                    V(lambda e, d=d, acol=acol, s=s: e.tensor_scalar(out=Ab[:, d, :], in0=Am1[:], scalar1=acol[:, s:s + 1], scalar2=1.0,
                                                                    op0=ALU.mult, op1=ALU.add), ["Am1", "alpha", "oma"], ["Ab%d" % d])
                    V(lambda e, acol=acol, s=s: e.tensor_scalar(out=tS[:], in0=PS["g"][:], scalar1=acol[:, s:s + 1], scalar2=None, op0=ALU.mult),
                      ["pg", "alpha", "oma"], ["tS"])
                    for h in range(4):
                        V(lambda e, d=d, h=h: e.scalar_tensor_tensor(out=Sst[:, 2, d, h, :], in0=Sst[:, 2, d, h, :], scalar=Ab[:, d, h:h + 1],
                                                                     in1=tS[:, h * 128:(h + 1) * 128], op0=ALU.mult, op1=ALU.add),
                          ["S2%d" % d, "Ab%d" % d, "tS"], ["S2%d" % d])
            if debug:
                dbg("S2", Sst[:, 2, :, :, :], [128, 2, 4, 128], ["S20", "S21"])
            V(lambda e: e.tensor_tensor(out=modT[:, 16:48, :], in0=PS["f"][:, 64:128].rearrange("p (c j) -> p c j", j=2),
                                        in1=b_adaT[:, 16:48].unsqueeze(2).to_broadcast([128, 32, 2]), op=ALU.add), ["pf2", "b_adaT"], ["modT2"])
            V(lambda e: e.scalar_tensor_tensor(out=G2T[:], in0=modT[:, 32:40, :], scalar=1.0, in1=n2T[:].unsqueeze(2).to_broadcast([128, 8, 2]),
                                               op0=ALU.add, op1=ALU.mult), ["modT2", "n2T"], ["G2T"])
            S.barrier()

        seq_tiles = [[0, 1], [2, 3], [4, 5, 6, 7]]
        tile_seq = {t: sq for sq, ts in enumerate(seq_tiles) for t in ts}
        with ExitStack() as st2:
            T2 = mk(st2)
            whg = T2("whg", [128, 8, 2560], BF16)
            zq_sb = T2("zq_sb", [128, 8, 512], BF16)
            v_sb = T2("v_sb", [128, 8, 512], BF16)
            ofw = T2("ofw", [128, 8, 512], BF16)
            f_ = T2("f_", [128, 512]); lf = T2("lf", [128, 512]); kk = T2("kk", [128, 512])
            eb = T2("eb", [128, 512]); enb = T2("enb", [128, 512]); ee2 = T2("ee2", [128, 512])
            qt = T2("qt", [128, 512], BF16); kt = T2("kt", [128, 512], BF16); kd = T2("kd", [128, 512], BF16)
            kdm = T2("kdm", [128, 4, 512], BF16)
            qkT = T2("qkT", [128, 8, 128], BF16)
            scm = T2("scm", [128, 4, 128], BF16)
            A_sb = T2("A_sb", [128, 16])
            Sbf = T2("Sbf", [128, 4, 4, 128], BF16)
            osq = T2("osq", [128, 512], BF16)

            for c0 in (0, 512, 1536, 1024, 2048):
                S.dma("gpsimd", whg[:, :, c0:c0 + 512], w_in_v[:, :, 1024 + c0:1024 + c0 + 512], writes=["whg%d" % (c0 // 512)])

            def hgrn_a(t, d):
                hT = hT_own[:, t, :, :]
                hk = "hT%d" % t
                if d == 0:
                    proj_tm(PS["e"][:], "pe", hT, hk, whg, "whg0", 0, 512)
                    A(lambda e: e.activation(out=zq_sb[:, t, :], in_=PS["e"][:], func=AF.Copy), ["pe"], ["zq%d" % t])
                    proj_tm(PS["e"][:], "pe", hT, hk, whg, "whg3", 1536, 512)
                    A(lambda e: e.activation(out=v_sb[:, t, :], in_=PS["e"][:], func=AF.Copy), ["pe"], ["v%d" % t])
                proj_tm(PS["d"][:], "pd", hT, hk, whg, "whg%d" % (1 + d), 512 + 512 * d, 512)
                A(lambda e: e.activation(out=f_[:], in_=PS["d"][:], func=AF.Exp, scale=-1.0), ["pd"], ["f_"])
                A(lambda e: e.activation(out=f_[:], in_=f_[:], func=AF.Ln, bias=1.0), ["f_"], ["f_"])
                A(lambda e: e.activation(out=f_[:], in_=f_[:], func=AF.Exp, scale=-1.0), ["f_"], ["f_"])
                V(lambda e: e.tensor_tensor(out=f_[:], in0=f_[:], in1=omlb_b[:, d, :], op=ALU.mult), ["f_"], ["f_"])
                V(lambda e: e.tensor_tensor(out=f_[:], in0=f_[:], in1=lb_b[:, d, :], op=ALU.add), ["f_"], ["f_"])
                A(lambda e: e.activation(out=lf[:], in_=f_[:], func=AF.Ln), ["f_"], ["lf"])
                G(lambda e: e.tensor_scalar(out=kk[:], in0=f_[:], scalar1=-1.0, scalar2=1.0, op0=ALU.mult, op1=ALU.add), ["f_"], ["kk"])

            def hgrn_b(t, d, nxt):
                sq = tile_seq[t]
                hT = hT_own[:, t, :, :]
                hk = "hT%d" % t
                skey = "S%d%d" % (sq, d)
                m1 = C("mfw") if d == 0 else C("mbw")
                m2 = C("m2fw") if d == 0 else C("m2bw")
                P(lambda e: e.matmul(PS["b"][:], lhsT=m1, rhs=lf[:], start=True, stop=True), ["cst", "lf"], ["pb"])
                P(lambda e: e.matmul(PS["c"][:], lhsT=m2, rhs=lf[:], start=True, stop=True), ["cst", "lf"], ["pc"])
                for h in range(4):
                    P(lambda e, h=h: e.matmul(PS["d"][:, h * 4:(h + 1) * 4], lhsT=lf[:, h * 128:(h + 1) * 128], rhs=C("ci"), start=True, stop=True),
                      ["cst", "lf"], ["pd"])
                A(lambda e: e.activation(out=eb[:], in_=PS["b"][:], func=AF.Exp), ["pb"], ["eb"])
                A(lambda e: e.activation(out=enb[:], in_=PS["b"][:], func=AF.Exp, scale=-1.0), ["pb"], ["enb"])
                A(lambda e: e.activation(out=ee2[:], in_=PS["c"][:], func=AF.Exp), ["pc"], ["ee2"])
                A(lambda e: e.activation(out=A_sb[:], in_=PS["d"][:, 0:16], func=AF.Exp), ["pd"], ["A_sb"])
                V(lambda e: e.tensor_tensor(out=qt[:], in0=zq_sb[:, t, :], in1=eb[:], op=ALU.mult), ["zq%d" % t, "eb"], ["qt"])
                V(lambda e: e.tensor_tensor(out=kt[:], in0=kk[:], in1=enb[:], op=ALU.mult), ["kk", "enb"], ["kt"])
                V(lambda e: e.tensor_tensor(out=kd[:], in0=kk[:], in1=ee2[:], op=ALU.mult), ["kk", "ee2"], ["kd"])
                V(lambda e: e.tensor_tensor(out=kdm[:], in0=kd[:].unsqueeze(1).to_broadcast([128, 4, 512]),
                                            in1=C("ci").unsqueeze(2).to_broadcast([128, 4, 512]), op=ALU.mult), ["kd", "cst"], ["kdm"])
                for h in range(4):
                    P(lambda e, h=h: e.transpose(out=PT[:, h * 128:(h + 1) * 128], in_=qt[:, h * 128:(h + 1) * 128], identity=ident_bf[:]),
                      ["qt", "ident_bf"], ["pt"])
                    P(lambda e, h=h: e.transpose(out=PT[:, (4 + h) * 128:(5 + h) * 128], in_=kt[:, h * 128:(h + 1) * 128], identity=ident_bf[:]),
                      ["kt", "ident_bf"], ["pt"])
                A(lambda e: e.activation(out=qkT[:].rearrange("p k t -> p (k t)"), in_=PT[:], func=AF.Copy), ["pt"], ["qkT"])
                for h in range(4):
                    P(lambda e, h=h: e.matmul(PS["e"][:, h * 128:(h + 1) * 128], lhsT=qkT[:, 4 + h, :], rhs=qkT[:, h, :], start=True, stop=True),
                      ["qkT"], ["pe"])
                V(lambda e: e.tensor_tensor(out=scm[:], in0=PS["e"][:].rearrange("p (h t) -> p h t", h=4),
                                            in1=m1.unsqueeze(1).to_broadcast([128, 4, 128]), op=ALU.mult), ["pe", "cst"], ["scm"])
                corder = [0, 1, 2, 3] if d == 0 else [3, 2, 1, 0]
                dbank = ["f", "g", "b", "c"]
                for h in range(4):
                    bank = dbank[h]
                    for c in range(4):
                        P(lambda e: e.matmul(PS[bank][:, c * 128:(c + 1) * 128], lhsT=kdm[:, c, h * 128:(h + 1) * 128],
                                             rhs=v_sb[:, t, h * 128:(h + 1) * 128], start=True, stop=True),
                          ["kdm", "v%d" % t], ["p" + bank])
                for c in corder:
                    for h in range(4):
                        bank = dbank[h]
                        hkey = skey + "_%d" % h
                        A(lambda e: e.activation(out=Sbf[:, h, c, :], in_=Sst[:, sq, d, h, :], func=AF.Copy), [hkey], ["Sbf%d_%d" % (h, c)])
                        V(lambda e: e.scalar_tensor_tensor(out=Sst[:, sq, d, h, :], in0=Sst[:, sq, d, h, :], scalar=A_sb[:, h * 4 + c:h * 4 + c + 1],
                                                           in1=PS[bank][:, c * 128:(c + 1) * 128], op0=ALU.mult, op1=ALU.add),
                          [hkey, "A_sb", "p" + bank], [hkey])
                if nxt is not None:
                    hgrn_a(*nxt)
                for h in range(4):
                    P(lambda e, h=h: e.matmul(PS["a"][:, h * 128:(h + 1) * 128], lhsT=v_sb[:, t, h * 128:(h + 1) * 128], rhs=scm[:, h, :],
                                              start=True, stop=False), ["v%d" % t, "scm"], ["pa"])
                    for c in range(4):
                        P(lambda e, h=h, c=c: e.matmul(PS["a"][:, h * 128 + c * 32:h * 128 + (c + 1) * 32], lhsT=Sbf[:, h, c, :],
                                                       rhs=qkT[:, h, c * 32:(c + 1) * 32], start=False, stop=(c == 3)),
                          ["Sbf%d_%d" % (h, c), "qkT"], ["pa"])
                if d == 0:
                    A(lambda e: e.activation(out=ofw[:, t, :], in_=PS["a"][:], func=AF.Copy), ["pa"], ["ofw%d" % t])
                else:
                    V(lambda e: e.tensor_tensor(out=eb[:], in0=PS["a"][:], in1=ofw[:, t, :], op=ALU.add), ["pa", "ofw%d" % t], ["eb"])
                    A(lambda e: e.activation(out=osq[:], in_=eb[:], func=AF.Square), ["eb"], ["osq"])
                    P(lambda e: e.matmul(PS["c"][:], lhsT=ones_bf[:], rhs=osq[:], start=True, stop=True), ["osq", "ones_bf"], ["pc"])
                    A(lambda e: e.activation(out=enb[:], in_=PS["c"][:], func=AF.Ln, scale=1.0 / 128.0, bias=EPS), ["pc"], ["enb"])
                    A(lambda e: e.activation(out=enb[:], in_=enb[:], func=AF.Exp, scale=-0.5), ["enb"], ["enb"])
                    for h in range(4):
                        proj_fm(PS["d"][:, h * 128:(h + 1) * 128], "pd", hT, hk, whg, "whg4", 2048 + h * 128)
                    A(lambda e: e.activation(out=ee2[:], in_=PS["d"][:], func=AF.Exp, scale=-1.0), ["pd"], ["ee2"])
                    A(lambda e: e.activation(out=ee2[:], in_=ee2[:], func=AF.Ln, bias=1.0), ["ee2"], ["ee2"])
                    A(lambda e: e.activation(out=ee2[:], in_=ee2[:], func=AF.Exp, scale=-1.0), ["ee2"], ["ee2"])
                    V(lambda e: e.tensor_tensor(out=ee2[:], in0=PS["d"][:], in1=ee2[:], op=ALU.mult), ["pd", "ee2"], ["ee2"])
                    V(lambda e: e.tensor_tensor(out=eb[:], in0=eb[:], in1=enb[:], op=ALU.mult), ["eb", "enb"], ["eb"])
                    for h in range(4):
                        V(lambda e, h=h: e.scalar_tensor_tensor(out=ybT[:, t, h, :], in0=eb[:, h * 128:(h + 1) * 128], scalar=hgT[:, h:h + 1],
                                                                in1=ee2[:, h * 128:(h + 1) * 128], op0=ALU.mult, op1=ALU.mult),
                          ["eb", "ee2", "hgT"], ["yb%d" % t])

            steps = [(t, 0) for t in range(8)] + [(t, 1) for ts in seq_tiles for t in reversed(ts)]
            hgrn_a(*steps[0])
            for si, (t, d) in enumerate(steps):
                hgrn_b(t, d, steps[si + 1] if si + 1 < len(steps) else None)
                if si == 7:
                    for sq in (0, 1):
                        S.dma("sync", st_d[sq, 0].rearrange("h k v -> k h v"), Sst[:, sq, 0, :, :], reads=["S%d0_%d" % (sq, h) for h in range(4)])
            for sq in (0, 1):
                S.dma("sync", st_d[sq, 1].rearrange("h k v -> k h v"), Sst[:, sq, 1, :, :], reads=["S%d1_%d" % (sq, h) for h in range(4)])
            if debug:
                dbg("yb0", ybT[:, 0, :, :], [128, 4, 128], ["yb0"], BF16)
                dbg("yb5", ybT[:, 5, :, :], [128, 4, 128], ["yb5"], BF16)
            S.barrier()
        L2.close()

        with ExitStack() as st3:
            T3 = mk(st3)
            gate1b = T3("gate1b", [128, 2, 1024]); G2b = T3("G2b", [128, 2, 1024], BF16); sh2b = T3("sh2b", [128, 2, 1024], BF16)
            h2t = [T3("h2t%d" % i, [128, 1024], BF16) for i in range(2)]
            for j in range(2):
                bcast_tile(gate1b[:, j, :], modT[:, 16:24, j], ["modT"], "gate1b%d" % j)
                bcast_tile(G2b[:, j, :], G2T[:, :, j], ["G2T"], "G2b%d" % j)
                bcast_tile(sh2b[:, j, :], modT[:, 24:32, j], ["modT"], "sh2b%d" % j)
            wuv = T3("wuv", [128, 8, 1024], BF16)
            wab = T3("wab", [128, 8, 2048], BF16)
            wpa = T3("wpa", [128, 4, 1024], BF16); wpb = T3("wpb", [128, 4, 1024], BF16)
            wo = T3("wo", [128, 8, 1024], BF16)
            uT = T3("uT", [128, 512], BF16)
            gv = T3("gv", [128, 512]); vhat = T3("vhat", [128, 512], BF16)
            vs = T3("vs", [128, 512]); yaT = [T3("yaT%d" % i, [128, 4, 128], BF16) for i in range(2)]
            sa = [T3("sa%d" % i, [128, 1024]) for i in range(2)]; sbb = [T3("sbb%d" % i, [128, 1024]) for i in range(2)]
            junkx = T3("junkx", [128, 512], BF16)
            t1 = T3("t1", [128, 1024]); mixT = T3("mixT", [128, 8, 128], BF16); mixb = T3("mixb", [128, 1024], BF16)
            xr = T3("xr", [128, 1024]); x1 = xr

            for c0 in range(0, 1024, 512):
                S.dma("gpsimd", wuv[:, :, c0:c0 + 512], w_in_v[:, :, c0:c0 + 512], writes=["wuv"])
            for c0 in range(0, 2048, 512):
                S.dma("gpsimd", wab[:, :, c0:c0 + 512], w_in_v[:, :, 3584 + c0:3584 + c0 + 512], writes=["wab"])
            S.dma("gpsimd", wpa[:], wpa_d.rearrange("(k p) c -> p k c", p=128), writes=["wpa"])
            S.dma("gpsimd", wpb[:], wpb_d.rearrange("(k p) c -> p k c", p=128), writes=["wpb"])
            S.dma("gpsimd", wo[:], wo_d.rearrange("(k p) c -> p k c", p=128), writes=["wo"])

            def stage_x1(t):
                p = t % 2
                hT = hT_own[:, t, :, :]
                hk = "hT%d" % t
                for g in range(4):
                    proj_fm(PS["a"][:, g * 128:(g + 1) * 128], "pa", hT, hk, wuv, "wuv", g * 128)
                A(lambda e: e.activation(out=uT[:], in_=PS["a"][:], func=AF.Gelu_apprx_tanh), ["pa"], ["uT"])
                proj_tm(PS["b"][:], "pb", hT, hk, wuv, "wuv", 512, 512)
                A(lambda e: e.activation(out=gv[:], in_=PS["b"][:], func=AF.Gelu_apprx_tanh), ["pb"], ["gv"])
                for half, (dst, nm) in enumerate([(sa[p], "sa%d_" % p), (sbb[p], "sbb%d_" % p)]):
                    for q in range(2):
                        bank = "d" if q == 0 else "e"
                        proj_tm(PS[bank][:], "p" + bank, hT, hk, wab, "wab", half * 1024 + q * 512, 512)
                        A(lambda e: e.activation(out=dst[:, q * 512:(q + 1) * 512], in_=PS[bank][:], func=AF.Sigmoid), ["p" + bank], [nm + str(q)])

            def stage_x2(t):
                p = t % 2
                A(lambda e: e.activation(out=junkx[:], in_=gv[:], func=AF.Square, accum_out=stat[:, 3:4]), ["gv"], ["junkx", "stat3"])
                A(lambda e: e.activation(out=stat[:, 4:5], in_=stat[:, 3:4], func=AF.Ln, scale=1.0 / 512.0, bias=EPS), ["stat3"], ["stat4"])
                A(lambda e: e.activation(out=stat[:, 5:6], in_=stat[:, 4:5], func=AF.Exp, scale=-0.5), ["stat4"], ["stat5"])
                V(lambda e: e.tensor_scalar(out=vhat[:], in0=gv[:], scalar1=stat[:, 5:6], scalar2=None, op0=ALU.mult), ["gv", "stat5"], ["vhat"])
                for g in range(4):
                    P(lambda e: e.matmul(PS["c"][:, g * 128:(g + 1) * 128], lhsT=vhat[:, g * 128:(g + 1) * 128], rhs=wsT[:, g, :],
                                         start=True, stop=True), ["vhat", "wsT"], ["pc"])
                for g in range(4):
                    V(lambda e: e.scalar_tensor_tensor(out=vs[:, g * 128:(g + 1) * 128], in0=PS["c"][:, g * 128:(g + 1) * 128],
                                                       scalar=sguT[:, g:g + 1], in1=bsb[:, g * 128:(g + 1) * 128], op0=ALU.mult, op1=ALU.add),
                      ["pc", "sguT", "bsb"], ["vs"])
                V(lambda e: e.tensor_tensor(out=yaT[p][:].rearrange("p g t -> p (g t)"), in0=uT[:], in1=vs[:], op=ALU.mult), ["uT", "vs"], ["yaT%d" % p])

            def stage_y1(t):
                p = t % 2
                j = 0 if t < 4 else 1
                S.dma("sync", xr[:], xres_d[t * 128:(t + 1) * 128, :], reads=["xres%d" % (t * 128)], writes=["xr"])
                for q in range(2):
                    bank = "f" if q == 0 else "g"
                    for ac in range(4):
                        P(lambda e: e.matmul(PS[bank][:], lhsT=yaT[p][:, ac, :], rhs=wpa[:, ac, q * 512:(q + 1) * 512], start=(ac == 0), stop=(ac == 3)),
                          ["wpa", "yaT%d" % p], ["p" + bank])
                    V(lambda e: e.tensor_tensor(out=t1[:, q * 512:(q + 1) * 512], in0=PS[bank][:], in1=sa[p][:, q * 512:(q + 1) * 512], op=ALU.mult),
                      ["p" + bank, "sa%d_%d" % (p, q)], ["t1_%d" % q])
                for q in range(2):
                    bank = "f" if q == 0 else "g"
                    for ac in range(4):
                        P(lambda e: e.matmul(PS[bank][:], lhsT=ybT[:, t, ac, :], rhs=wpb[:, ac, q * 512:(q + 1) * 512], start=(ac == 0), stop=(ac == 3)),
                          ["wpb", "yb%d" % t], ["p" + bank])
                    V(lambda e: e.tensor_tensor(out=sbb[p][:, q * 512:(q + 1) * 512], in0=PS[bank][:], in1=sbb[p][:, q * 512:(q + 1) * 512], op=ALU.mult),
                      ["p" + bank, "sbb%d_%d" % (p, q)], ["sbb%d_%d" % (p, q)])
                    V(lambda e: e.tensor_tensor(out=mixb[:, q * 512:(q + 1) * 512], in0=t1[:, q * 512:(q + 1) * 512], in1=sbb[p][:, q * 512:(q + 1) * 512],
                                                op=ALU.add), ["t1_%d" % q, "sbb%d_%d" % (p, q)], ["mixb%d" % q])
                for k in range(8):
                    P(lambda e: e.transpose(out=PT[:, k * 128:(k + 1) * 128], in_=mixb[:, k * 128:(k + 1) * 128], identity=ident_bf[:]),
                      ["mixb%d" % (k // 4), "ident_bf"], ["pt"])
                A(lambda e: e.activation(out=mixT[:].rearrange("p k t -> p (k t)"), in_=PT[:], func=AF.Copy), ["pt"], ["mixT"])

            def stage_y2(t):
                p = t % 2
                j = 0 if t < 4 else 1
                for q in range(2):
                    bank = "f" if q == 0 else "g"
                    for k in range(8):
                        P(lambda e: e.matmul(PS[bank][:], lhsT=mixT[:, k, :], rhs=wo[:, k, q * 512:(q + 1) * 512], start=(k == 0), stop=(k == 7)),
                          ["mixT", "wo"], ["p" + bank])
                    V(lambda e: e.tensor_tensor(out=t1[:, q * 512:(q + 1) * 512], in0=PS[bank][:], in1=gate1b[:, j, q * 512:(q + 1) * 512], op=ALU.mult),
                      ["p" + bank], ["t1_%d" % q])
                V(lambda e: e.tensor_tensor(out=x1[:], in0=t1[:], in1=xr[:], op=ALU.add), ["t1_0", "t1_1", "xr"], ["xr"])
                S.dma("sync", x1s_d[t * 128:(t + 1) * 128, :], x1[:], reads=["xr"], writes=["x1s%d" % t])
                if debug and t in (0, 5):
                    dbg("x1_%d" % t, x1[:], [128, 1024], ["xr"])
                h2c = h2t[t % 2]
                hkey = "h2t%d" % (t % 2)
                A(lambda e: e.activation(out=junkb[:], in_=x1[:], func=AF.Square, accum_out=stat[:, 0:1]), ["xr"], ["junkb", "stat0"])
                A(lambda e: e.activation(out=stat[:, 1:2], in_=stat[:, 0:1], func=AF.Ln, scale=1.0 / 1024.0, bias=EPS), ["stat0"], ["stat1"])
                A(lambda e: e.activation(out=stat[:, 2:3], in_=stat[:, 1:2], func=AF.Exp, scale=-0.5), ["stat1"], ["stat2"])
                V(lambda e: e.scalar_tensor_tensor(out=tmpf[:], in0=x1[:], scalar=stat[:, 2:3], in1=G2b[:, j, :], op0=ALU.mult, op1=ALU.mult),
                  ["xr", "stat2"], ["tmpf"])
                V(lambda e: e.tensor_tensor(out=h2c[:], in0=tmpf[:], in1=sh2b[:, j, :], op=ALU.add), ["tmpf"], [hkey])

            def stage_y3(t):
                h2c = h2t[t % 2]
                hkey = "h2t%d" % (t % 2)
                for k in range(8):
                    P(lambda e: e.transpose(out=PT[:, k * 128:(k + 1) * 128], in_=h2c[:, k * 128:(k + 1) * 128], identity=ident_bf[:]),
                      [hkey, "ident_bf"], ["pt"])
                A(lambda e: e.activation(out=h2T[:, t, :, :].rearrange("p k t -> p (k t)"), in_=PT[:], func=AF.Copy), ["pt"], ["h2T%d" % t])

            stage_x1(0)
            stage_x2(0)
            for t in range(8):
                stage_y1(t)
                if t + 1 < 8:
                    stage_x1(t + 1)
                stage_y2(t)
                if t + 1 < 8:
                    stage_x2(t + 1)
                stage_y3(t)
            S.barrier()
        L1.close()

        with ExitStack() as st4:
            T4 = mk(st4)
            wq = T4("wq", [128, 8, 2048], BF16)
            qT_sb = T4("qT_sb", [128, 16, 128], BF16)
            sc2 = [T4("sc_sb%d" % i, [128, 16, 128]) for i in range(2)]
            v16 = T4("v16", [128, 16, 16]); i16u = T4("i16u", [128, 16, 16], U32); i16f = T4("i16f", [128, 16, 16])
            wk = T4("wk", [128, 16, 128])
            cand = T4("cand", [128, 8, 16, 16]); cwk = T4("cwk", [128, 8, 256])
            tv = T4("tv", [128, 8, 16]); ju = T4("ju", [128, 8, 16], U32)
            au = T4("au", [128, 8, 16], U32); bu = T4("bu", [128, 8, 16], U32)
            af = T4("af", [128, 8, 16]); bf = T4("bf", [128, 8, 16])
            eq = T4("eq", [128, 8, 16, 16])
            i1s = T4("i1s", [128, 8, 16]); i2s = T4("i2s", [128, 8, 16])
            ev = T4("ev", [128, 8, 16]); zs = T4("zs", [128, 8]); gg = T4("gg", [128, 8, 16])
            skT = T4("skT", [128, 16, 128], BF16); skl = T4("skl", [128, 16, 128])
            slT = T4("slT", [128, 384])
            oj = [T4("oj%d" % i, [128, 16, 128], BF16) for i in range(2)]
            oi = [T4("oi%d" % i, [128, 16, 128], BF16) for i in range(2)]
            slb = T4("slb", [128, 384], BF16)
            ub = [T4("ub%d" % i, [128, 1024], BF16) for i in range(3)]
            glb = [T4("glb%d" % i, [128, 1024], BF16) for i in range(2)]
            iota_bf = T4("iota_bf", [128, 128], BF16)
            V(lambda e: e.tensor_copy(out=iota_bf[:], in_=C("iota128")), ["cst"], ["iota_bf"])
            Gst = T4("Gst", [128, 128, 128], BF16)
            S.dma("sync", skl[:], sk_d.rearrange("g k c -> k g c"), writes=["skl"])
            for q4 in range(4):
                bank = "defg"[q4]
                for i in range(4):
                    hp = q4 * 4 + i
                    P(lambda e: e.transpose(out=PS[bank][:, i * 128:(i + 1) * 128], in_=skl[:, hp, :], identity=C("ident")),
                      ["skl", "cst"], ["p" + bank])
                A(lambda e: e.activation(out=skT[:, q4 * 4:(q4 + 1) * 4, :].rearrange("p g t -> p (g t)"), in_=PS[bank][:], func=AF.Copy),
                  ["p" + bank], ["skT"])
            wq_v = wq_d.rearrange("(k p) c -> p k c", p=128)
            for c0 in range(0, 2048, 512):
                S.dma("gpsimd", wq[:, :, c0:c0 + 512], wq_v[:, :, c0:c0 + 512], writes=["wq"])

            def first_stage(i):
                u3 = i % 3
                S.dma("gpsimd", ub[u3][:], pu_d[i], writes=["ub%d" % u3])
                for half in range(2):
                    bank = "c" if half == 0 else "d"
                    for k in range(8):
                        P(lambda e: e.matmul(PS[bank][:], lhsT=ub[u3][:, k * 128:(k + 1) * 128], rhs=h2T[:, half * 4:(half + 1) * 4, k, :],
                                             start=(k == 0), stop=(k == 7)), ["ub%d" % u3], ["p" + bank])
                    A(lambda e: e.activation(out=glb[i % 2][:, half * 512:(half + 1) * 512], in_=PS[bank][:], func=AF.Gelu_apprx_tanh),
                      ["p" + bank], ["glb%d_%d" % (i % 2, half)])
                S.dma("sync", gls_d[i], glb[i % 2][:], reads=["glb%d_0" % (i % 2), "glb%d_1" % (i % 2)], writes=["gls"])

            def stage_a(t):
                hT = h2T[:, t, :, :]
                hk = "h2T%d" % t
                scb = sc2[t % 2]
                for q4 in range(4):
                    bank = "ab"[q4 % 2]
                    for i in range(4):
                        proj_fm(PS[bank][:, i * 128:(i + 1) * 128], "p" + bank, hT, hk, wq, "wq", (q4 * 4 + i) * 128)
                    A(lambda e: e.activation(out=qT_sb[:, q4 * 4:(q4 + 1) * 4, :].rearrange("p g t -> p (g t)"), in_=PS[bank][:],
                                             func=AF.Copy), ["p" + bank], ["qT_sb%d" % q4])
                for q4 in range(4):
                    bank = "ab"[q4 % 2]
                    for i in range(4):
                        hp = q4 * 4 + i
                        P(lambda e: e.matmul(PS[bank][:, i * 128:(i + 1) * 128], lhsT=qT_sb[:, hp, :], rhs=skT[:, hp, :],
                                             start=True, stop=True), ["qT_sb%d" % q4, "skT"], ["p" + bank])
                    A(lambda e: e.activation(out=scb[:, q4 * 4:(q4 + 1) * 4, :].rearrange("p g t -> p (g t)"), in_=PS[bank][:],
                                             func=AF.Copy), ["p" + bank], ["sc%d_%d" % (t % 2, q4)])

            stage_a(0)
            for t in range(8):
                if t + 1 < 8:
                    stage_a(t + 1)
                for i8 in range(8):
                    first_stage(t * 16 + i8)
                sc_sb = sc2[t % 2]
                for hp in range(16):
                    V(lambda e: e.max(out=v16[:, hp, 0:8], in_=sc_sb[:, hp, :]), ["sc%d_%d" % (t % 2, hp // 4)], ["v16a%d" % hp])
                for hp in range(16):
                    V(lambda e: e.max_index(out=i16u[:, hp, 0:8], in_max=v16[:, hp, 0:8], in_values=sc_sb[:, hp, :]),
                      ["sc%d_%d" % (t % 2, hp // 4), "v16a%d" % hp], ["i16ua%d" % hp])
                for hp in range(16):
                    V(lambda e: e.match_replace(out=wk[:, hp, :], in_to_replace=v16[:, hp, 0:8], in_values=sc_sb[:, hp, :], imm_value=NEG),
                      ["sc%d_%d" % (t % 2, hp // 4), "v16a%d" % hp], ["wk%d" % hp])
                for hp in range(16):
                    V(lambda e: e.max(out=v16[:, hp, 8:16], in_=wk[:, hp, :]), ["wk%d" % hp], ["v16b%d" % hp])
                for hp in range(16):
                    V(lambda e: e.max_index(out=i16u[:, hp, 8:16], in_max=v16[:, hp, 8:16], in_values=wk[:, hp, :]), ["wk%d" % hp, "v16b%d" % hp],
                      ["i16ub%d" % hp])
                V(lambda e: e.tensor_copy(out=i16f[:], in_=i16u[:]), ["i16ua%d" % i for i in range(16)] + ["i16ub%d" % i for i in range(16)], ["i16f"])
                v16r = v16[:].rearrange("p (h two) a -> p h two a", two=2)
                i16r = i16f[:].rearrange("p (h two) a -> p h two a", two=2)
                V(lambda e: e.tensor_tensor(out=cand[:], in0=v16r[:, :, 0, :].unsqueeze(3).to_broadcast([128, 8, 16, 16]),
                                            in1=v16r[:, :, 1, :].unsqueeze(2).to_broadcast([128, 8, 16, 16]), op=ALU.add), ["v16a%d" % i for i in range(16)] + ["v16b%d" % i for i in range(16)], ["cand"])
                chs = [cand[:, h, :, :].rearrange("p a b -> p (a b)") for h in range(8)]
                for h in range(8):
                    V(lambda e: e.max(out=tv[:, h, 0:8], in_=chs[h]), ["cand"], ["tva%d" % h])
                for h in range(8):
                    V(lambda e: e.max_index(out=ju[:, h, 0:8], in_max=tv[:, h, 0:8], in_values=chs[h]), ["cand", "tva%d" % h], ["jua%d" % h])
                for h in range(8):
                    V(lambda e: e.match_replace(out=cwk[:, h, :], in_to_replace=tv[:, h, 0:8], in_values=chs[h], imm_value=NEG),
                      ["cand", "tva%d" % h], ["cwk%d" % h])
                for h in range(8):
                    V(lambda e: e.max(out=tv[:, h, 8:16], in_=cwk[:, h, :]), ["cwk%d" % h], ["tvb%d" % h])
                for h in range(8):
                    V(lambda e: e.max_index(out=ju[:, h, 8:16], in_max=tv[:, h, 8:16], in_values=cwk[:, h, :]), ["cwk%d" % h, "tvb%d" % h], ["jub%d" % h])
                V(lambda e: e.tensor_single_scalar(out=au[:], in_=ju[:], scalar=4, op=ALU.logical_shift_right), ["jua%d" % i for i in range(8)] + ["jub%d" % i for i in range(8)], ["au"])
                V(lambda e: e.tensor_single_scalar(out=bu[:], in_=ju[:], scalar=15, op=ALU.bitwise_and), ["jua%d" % i for i in range(8)] + ["jub%d" % i for i in range(8)], ["bu"])
                V(lambda e: e.tensor_copy(out=af[:], in_=au[:]), ["au"], ["af"])
                V(lambda e: e.tensor_copy(out=bf[:], in_=bu[:]), ["bu"], ["bf"])
                io = C("iota16").unsqueeze(1).unsqueeze(1).to_broadcast([128, 8, 16, 16])
                for (src, two, dst, nm) in [(af, 0, i1s, "i1s"), (bf, 1, i2s, "i2s")]:
                    V(lambda e: e.tensor_tensor(out=eq[:], in0=src[:].unsqueeze(3).to_broadcast([128, 8, 16, 16]), in1=io, op=ALU.is_equal),
                      ["af", "bf", "cst"], ["eq"])
                    V(lambda e: e.tensor_tensor(out=eq[:], in0=eq[:], in1=i16r[:, :, two, :].unsqueeze(2).to_broadcast([128, 8, 16, 16]),
                                                op=ALU.mult), ["eq", "i16f"], ["eq"])
                    V(lambda e: e.tensor_reduce(out=dst[:], in_=eq[:], axis=AX.X, op=ALU.add), ["eq"], [nm])
                V(lambda e: e.tensor_tensor(out=ev[:], in0=tv[:], in1=tv[:, :, 0:1].to_broadcast([128, 8, 16]), op=ALU.subtract), ["tva%d" % i for i in range(8)] + ["tvb%d" % i for i in range(8)], ["ev"])
                A(lambda e: e.activation(out=ev[:], in_=ev[:], func=AF.Exp), ["ev"], ["ev"])
                V(lambda e: e.tensor_reduce(out=zs[:], in_=ev[:], axis=AX.X, op=ALU.add), ["ev"], ["zs"])
                V(lambda e: e.reciprocal(out=zs[:], in_=zs[:]), ["zs"], ["zs"])
                V(lambda e: e.tensor_tensor(out=gg[:], in0=ev[:], in1=zs[:].unsqueeze(2).to_broadcast([128, 8, 16]), op=ALU.mult), ["ev", "zs"], ["gg"])
                for n, (src, nm) in enumerate([(i1s, "i1s"), (i2s, "i2s"), (gg, "gg")]):
                    P(lambda e: e.transpose(out=PS["e"][:, n * 128:(n + 1) * 128], in_=src[:].rearrange("p h k -> p (h k)"), identity=C("ident")),
                      [nm, "cst"], ["pe"])
                A(lambda e: e.activation(out=slT[:], in_=PS["e"][:, 0:384], func=AF.Copy), ["pe"], ["slT"])
                V(lambda e: e.tensor_copy(out=slb[:], in_=slT[:]), ["slT"], ["slb"])
                for tb in range(8):
                    ob = tb % 2
                    t0 = tb * 16
                    first_stage(t * 16 + 8 + tb)
                    iob = iota_bf[:].unsqueeze(1).to_broadcast([128, 16, 128])
                    V(lambda e: e.tensor_tensor(out=oi[ob][:], in0=iob, in1=slb[:, t0:t0 + 16].unsqueeze(2).to_broadcast([128, 16, 128]),
                                                op=ALU.is_equal), ["iota_bf", "slb"], ["oi%d" % ob])
                    V(lambda e: e.tensor_tensor(out=oj[ob][:], in0=iob, in1=slb[:, 128 + t0:128 + t0 + 16].unsqueeze(2).to_broadcast([128, 16, 128]),
                                                op=ALU.is_equal), ["iota_bf", "slb"], ["oj%d" % ob])
                    V(lambda e: e.tensor_tensor(out=oj[ob][:], in0=oj[ob][:], in1=slb[:, 256 + t0:256 + t0 + 16].unsqueeze(2).to_broadcast([128, 16, 128]),
                                                op=ALU.mult), ["oj%d" % ob, "slb"], ["oj%d" % ob])
                    for q in range(16):
                        tok = t0 + q
                        r4 = tok % 4
                        bank = "f" if (tok // 4) % 2 == 0 else "g"
                        P(lambda e: e.matmul(PS[bank][:, r4 * 128:(r4 + 1) * 128], lhsT=oj[ob][:, q, :], rhs=oi[ob][:, q, :], start=True, stop=True),
                          ["oj%d" % ob, "oi%d" % ob], ["p" + bank])
                        if r4 == 3:
                            A(lambda e: e.activation(out=Gst[:, :, tok - 3:tok + 1], in_=PS[bank][:].rearrange("p (q i) -> p i q", q=4), func=AF.Copy),
                              ["p" + bank], ["Gst"])
                for i0 in range(0, 128, 16):
                    S.dma("sync", gd_d[i0:i0 + 16, :, t * 128:(t + 1) * 128].rearrange("i j t -> j i t"), Gst[:, i0:i0 + 16, :],
                          reads=["Gst"], writes=["gd"])
            S.barrier()

        with ExitStack() as st5:
            T5 = mk(st5)
            acc = T5("acc", [128, 8, 1024])
            Ag = [T5("Ag%d" % i, [128, 8, 1024], BF16) for i in range(2)]
            vg = [T5("vg%d" % i, [128, 8, 1024], BF16) for i in range(2)]
            glr = [T5("glr%d" % i, [128, 1024], BF16) for i in range(3)]
            gt = [T5("gt%d" % i, [128, 1024], BF16) for i in range(3)]
            gate2b = T5("gate2b", [128, 2, 1024]); fgb = T5("fgb", [128, 1024])
            x1r = T5("x1r", [128, 1024]); x2 = T5("x2", [128, 1024]); yo = T5("yo", [128, 1024])
            S.dma("sync", fgb[:], fg_d.partition_broadcast(128), writes=["fgb"])
            for j in range(2):
                bcast_tile(gate2b[:, j, :], modT[:, 40:48, j], ["modT"], "gate2b%d" % j)
            def prep_one(grp, e8):
                gp = grp % 2
                if True:
                    i = grp * 8 + e8
                    u3 = i % 3
                    S.dma("sync", glr[u3][:], gls_d[i], writes=["glr%d" % u3])
                    S.dma("gpsimd", vg[gp][:, e8, :], pv_d[i * 128:(i + 1) * 128, :], writes=["vg%d_%d" % (gp, e8)])
                    S.dma("scalar", gt[u3][:], gd_d[i], writes=["gt%d" % u3])
                    V(lambda e: e.tensor_tensor(out=Ag[gp][:, e8, :], in0=glr[u3][:], in1=gt[u3][:], op=ALU.mult),
                      ["glr%d" % u3, "gt%d" % u3], ["Ag%d_%d_0" % (gp, e8), "Ag%d_%d_1" % (gp, e8)])

            def second_stage(grp):
                gp = grp % 2
                for tile in range(8):
                    if grp + 1 < 16:
                        prep_one(grp + 1, tile)
                    banks = ("d", "e") if tile % 2 == 0 else ("f", "g")
                    for e8 in range(8):
                        for half in range(2):
                            P(lambda e: e.matmul(PS[banks[half]][:], lhsT=Ag[gp][:, e8, tile * 128:(tile + 1) * 128],
                                                 rhs=vg[gp][:, e8, half * 512:(half + 1) * 512], start=(e8 == 0), stop=(e8 == 7)),
                              ["Ag%d_%d_%d" % (gp, e8, tile // 4), "vg%d_%d" % (gp, e8)], ["p" + banks[half]])
                    for half in range(2):
                        dst = acc[:, tile, half * 512:(half + 1) * 512]
                        akey = "acc%d_%d" % (tile, half)
                        if grp == 0:
                            V(lambda e: e.tensor_copy(out=dst, in_=PS[banks[half]][:]), ["p" + banks[half]], [akey])
                        else:
                            V(lambda e: e.tensor_tensor(out=dst, in0=PS[banks[half]][:], in1=dst, op=ALU.add), ["p" + banks[half], akey], [akey])

            for e8_ in range(8):
                prep_one(0, e8_)
            for grp in range(16):
                second_stage(grp)
            for t in range(8):
                j = 0 if t < 4 else 1
                S.dma("sync", x1r[:], x1s_d[t * 128:(t + 1) * 128, :], writes=["x1r"])
                V(lambda e: e.tensor_tensor(out=x2[:], in0=acc[:, t, :], in1=gate2b[:, j, :], op=ALU.mult), ["acc%d_0" % t, "acc%d_1" % t], ["x2"])
                V(lambda e: e.tensor_tensor(out=x2[:], in0=x2[:], in1=x1r[:], op=ALU.add), ["x2", "x1r"], ["x2"])
                A(lambda e: e.activation(out=junkb[:], in_=x2[:], func=AF.Square, accum_out=stat[:, 0:1]), ["x2"], ["junkb", "stat0"])
                A(lambda e: e.activation(out=stat[:, 1:2], in_=stat[:, 0:1], func=AF.Ln, scale=1.0 / 1024.0, bias=EPS), ["stat0"], ["stat1"])
                A(lambda e: e.activation(out=stat[:, 2:3], in_=stat[:, 1:2], func=AF.Exp, scale=-0.5), ["stat1"], ["stat2"])
                V(lambda e: e.scalar_tensor_tensor(out=yo[:], in0=x2[:], scalar=stat[:, 2:3], in1=fgb[:], op0=ALU.mult, op1=ALU.mult),
                  ["x2", "stat2", "fgb"], ["yo"])
                dsto = yp_d[t * 128:(t + 1) * 128, :] if t < 4 else ys_d[(t - 4) * 128:(t - 3) * 128, :]
                S.dma("sync", dsto, yo[:], reads=["yo"])
            S.finish()
            S.emit()
    return nc


_PROGRAM = {}


def _prep_inputs(inp):
    f = lambda a: np.ascontiguousarray(np.asarray(a, dtype=np.float32))
    x_prompt = f(inp["x_prompt"]); x_sample = f(inp["x_sample"]); state = f(inp["state_hgrn"])
    c = f(inp["c"]); c_ctx = f(inp["c_ctx"])
    w_in = f(inp["w_in"])[0]
    hgrn_lb = f(inp["hgrn_lb"])
    shared = {
        "cst": _CST,
        "w_ada": f(inp["w_ada"])[0],
        "b_adaT": f(f(inp["b_ada"])[0].reshape(48, 128).T),
        "n1T": f(f(inp["norm1_g"])[0].reshape(8, 128).T),
        "n2T": f(f(inp["norm2_g"])[0].reshape(8, 128).T),
        "w_in": w_in,
        "sguT": f(f(inp["sgu_norm_g"])[0].reshape(4, 128).T),
        "w_sp": f(inp["w_spatial"])[0],
        "b_sp": f(f(inp["b_spatial"])[0].reshape(1, 512)),
        "lb": f(hgrn_lb.reshape(1, 2048)),
        "hgT": f(f(inp["hgrn_norm_g"])[0].T),
        "wpa": f(inp["w_proj_a"])[0], "wpb": f(inp["w_proj_b"])[0], "wo": f(inp["w_out"])[0],
        "wq": f(inp["peer_w_q"])[0],
        "sk": f(f(inp["peer_sub_keys"])[0].reshape(16, 128, 128)),
        "pu": f(f(inp["peer_u"])[0].reshape(128, 128, 8, 128).transpose(0, 3, 2, 1).reshape(128, 128, 1024)),
        "pv": f(inp["peer_v"])[0],
        "fg": f(f(inp["final_norm_g"]).reshape(1, 1024)),
    }
    wf = [w_in[:, 1536:2048], w_in[:, 2048:2560]]
    maps = []
    for core in range(8):
        b, j = core // 4, core % 4
        m = dict(shared)
        m["xp"] = f(x_prompt[2 * core:2 * core + 2].reshape(512, 1024))
        m["xs"] = f(x_sample[b, 512 * j:512 * (j + 1)])
        slots = [(s, 0) for s in range(0, j)] + [(s, 1) for s in range(3, j, -1)]
        xo = np.zeros((3, 512, 1024), np.float32)
        tok_idx = np.zeros((16, 128), np.int64)
        for tt in range(4):
            tok_idx[tt] = 512 * j + tt * 128 + np.arange(128)
        wfs = np.zeros((3, 1024, 512), np.float32)
        lbs = np.zeros((3, 2, 512), np.float32)
        alpha = np.zeros((128, 3), np.float32)
        for si, (seg, d) in enumerate(slots):
            toks = 512 * seg + np.arange(512)
            if d == 1:
                toks = toks[::-1]
            xo[si] = x_sample[b, toks]
            for tt in range(4):
                tok_idx[4 + si * 4 + tt] = toks[tt * 128:(tt + 1) * 128]
            wfs[si] = wf[d]
            lbs[si] = hgrn_lb[:, d, :]
            alpha[:, si] = 1.0 if d == 0 else 0.0
        sel = np.zeros((16, 2, 128, 128), np.float32)
        ar = np.arange(128)
        for ti in range(16):
            sel[ti, 0, tok_idx[ti] // 64, ar] = 1.0
            sel[ti, 1, tok_idx[ti] % 64, ar] = 1.0
        m["xo"] = xo; m["sel"] = sel; m["wfs"] = wfs; m["lbs"] = f(lbs.reshape(1, 3072)); m["alpha"] = alpha
        m["s0"] = f(state[b, 0])
        cond2 = np.stack([c_ctx, c[b]], axis=0)
        m["condT"] = f(cond2.reshape(2, 8, 128).transpose(2, 1, 0).reshape(128, 16))
        maps.append(m)
    return maps


def run(inp, debug=False):
    key = bool(debug)
    if key not in _PROGRAM:
        _PROGRAM[key] = build_program(debug)
    nc = _PROGRAM[key]
    maps = _prep_inputs(inp)
    res = run_bass_kernel_spmd(nc, maps, core_ids=list(range(8)))
    return res.results


def kernel(**inputs):
    rs = run(inputs, debug=False)
    y_prompt = np.concatenate([r["yp"].reshape(2, 256, 1024) for r in rs], axis=0).astype(np.float32)
    y_sample = np.stack([np.concatenate([rs[b * 4 + j]["ys"] for j in range(4)], axis=0) for b in range(2)], axis=0).astype(np.float32)
    st = np.concatenate([r["st"].reshape(2, 1, 2, 4, 128, 128) for r in rs], axis=0).astype(np.float32)
    return (y_prompt, y_sample, st)
```

```python
import math
import types
from contextlib import ExitStack
import numpy as np
import concourse.bass as bass
import concourse.mybir as mybir
from concourse.bass_utils import run_bass_kernel_spmd

F32 = mybir.dt.float32
BF16 = mybir.dt.bfloat16
I32 = mybir.dt.int32
U32 = mybir.dt.uint32
AF = mybir.ActivationFunctionType
ALU = mybir.AluOpType
AX = mybir.AxisListType

ENGS = ["tensor", "vector", "scalar", "gpsimd", "sync"]
N_DMA_SLOTS = 14
EPS = 1e-6
NEG = -1.0e30


def _snap(fn):
    if fn.__closure__ is None:
        return fn
    cells = tuple(types.CellType(c.cell_contents) for c in fn.__closure__)
    return types.FunctionType(fn.__code__, fn.__globals__, fn.__name__, fn.__defaults__, cells)


class Sched:
    def __init__(self, nc, stack):
        self.nc = nc
        self.prog = {e: [] for e in ENGS}
        self.count = {e: 0 for e in ENGS}
        self.sem = {e: stack.enter_context(nc.semaphore("s_" + e)) for e in ENGS if e != "sync"}
        self.dsem = [stack.enter_context(nc.semaphore("d_%d" % i)) for i in range(N_DMA_SLOTS)]
        self.dcount = [0] * N_DMA_SLOTS
        self.dnext = 0
        self.seen = {e: {} for e in ENGS}
        self.snap = {}
        self.bufs = {}
        self.pool_out = []

    def _deps(self, reads, writes):
        deps = []
        for k in reads:
            b = self.bufs.get(k)
            if b and b["w"] is not None:
                deps.append(b["w"])
        for k in writes:
            b = self.bufs.get(k)
            if b:
                if b["w"] is not None:
                    deps.append(b["w"])
                deps.extend(b["r"])
        return deps

    def _record(self, tok, reads, writes):
        for k in reads:
            b = self.bufs.setdefault(k, {"w": None, "r": []})
            b["r"].append(tok)
        for k in writes:
            self.bufs[k] = {"w": tok, "r": []}

    def _emit_waits(self, eng, deps):
        seen = self.seen[eng]
        need = {}
        for key, val in deps:
            if key == eng and eng == "tensor":
                continue
            if seen.get(key, 0) >= val:
                continue
            if need.get(key, 0) < val:
                need[key] = val
        for key, val in sorted(need.items(), key=lambda kv: -kv[1]):
            if seen.get(key, 0) >= val:
                continue
            sem = self.dsem[key[1]] if isinstance(key, tuple) else self.sem[key]
            self.prog[eng].append(("wait", sem, val))
            seen[key] = val
            sn = self.snap.get((key, val))
            if sn:
                for k2, v2 in sn.items():
                    if seen.get(k2, 0) < v2:
                        seen[k2] = v2

    def op(self, eng, fn, reads=(), writes=()):
        self._emit_waits(eng, self._deps(reads, writes))
        self.count[eng] += 1
        idx = self.count[eng]
        self.prog[eng].append(("op", _snap(fn)))
        tok = (eng, idx)
        self.snap[tok] = dict(self.seen[eng])
        self._record(tok, reads, writes)
        return tok

    def dma(self, eng, out, in_, reads=(), writes=(), **kw):
        return self.dma_fn(eng, lambda e: e.dma_start(out=out, in_=in_, **kw), reads, writes)

    def dma_fn(self, eng, fn, reads=(), writes=()):
        deps = self._deps(reads, writes)
        slot = self.dnext
        self.dnext = (self.dnext + 1) % N_DMA_SLOTS
        key = ("d", slot)
        if self.dcount[slot] > 0:
            deps.append((key, self.dcount[slot]))
        if eng == "gpsimd":
            if len(self.pool_out) >= 6:
                deps.append(self.pool_out.pop(0))
        self._emit_waits(eng, deps)
        self.dcount[slot] += 16
        val = self.dcount[slot]
        self.prog[eng].append(("dma", _snap(fn), self.dsem[slot]))
        tok = (key, val)
        if eng == "gpsimd":
            self.pool_out.append(tok)
        self.snap[tok] = dict(self.seen[eng])
        self._record(tok, reads, writes)
        return tok

    def _all_tokens(self):
        deps = [(e, self.count[e]) for e in ENGS if e != "sync" and self.count[e] > 0]
        deps += [(("d", s), self.dcount[s]) for s in range(N_DMA_SLOTS) if self.dcount[s] > 0]
        return deps

    def barrier(self):
        deps = self._all_tokens()
        for e in ENGS:
            self._emit_waits(e, deps)
        self.bufs = {}

    def finish(self):
        self._emit_waits("sync", self._all_tokens())

    def emit(self):
        nc = self.nc
        with nc.Block() as block:
            def runner(name):
                def _run(e):
                    for item in self.prog[name]:
                        if item[0] == "wait":
                            e.wait_ge(item[1], item[2])
                        elif item[0] == "op":
                            item[1](e).then_inc(self.sem[name], 1)
                        else:
                            item[1](e).then_inc(item[2], 16)
                return _run
            block.tensor(runner("tensor"))
            block.vector(runner("vector"))
            block.scalar(runner("scalar"))
            block.gpsimd(runner("gpsimd"))
            block.sync(runner("sync"))


CST_LAYOUT = {}


def make_consts():
    r = np.arange(128)
    same = (r[:, None] // 32) == (r[None, :] // 32)
    parts = [
        ("ident", np.eye(128)),
        ("mfw", same & (r[:, None] <= r[None, :])),
        ("mbw", same & (r[:, None] >= r[None, :])),
        ("m2fw", same & (r[:, None] > r[None, :])),
        ("m2bw", same & (r[:, None] < r[None, :])),
        ("tris", r[:, None] > r[None, :]),
        ("ones", np.ones((128, 128))),
        ("ci", (r[:, None] // 32) == np.arange(4)[None, :]),
        ("iota16", np.broadcast_to(np.arange(16)[None, :], (128, 16))),
        ("freqidx", np.broadcast_to(np.arange(256)[None, :], (128, 256))),
        ("pcol", r[:, None]),
        ("iota128", np.broadcast_to(np.arange(128)[None, :], (128, 128))),
    ]
    off = 0
    cols = []
    for name, a in parts:
        a = np.asarray(a, dtype=np.float32)
        CST_LAYOUT[name] = (off, a.shape[1])
        off += a.shape[1]
        cols.append(a)
    return np.ascontiguousarray(np.concatenate(cols, axis=1))


_CST = make_consts()
NCST = _CST.shape[1]

DEBUG_OUTS = {}


def build_program(debug=False):
    nc = bass.Bass("TRN2", target_bir_lowering=False)

    def din(name, shape, dt=F32):
        return nc.dram_tensor(name, list(shape), dt, kind="ExternalInput").ap()

    def dout(name, shape, dt=F32):
        return nc.dram_tensor(name, list(shape), dt, kind="ExternalOutput").ap()

    xp_d = din("xp", [512, 1024]); xs_d = din("xs", [512, 1024]); xo_d = din("xo", [3, 512, 1024])
    sel_d = din("sel", [16, 2, 128, 128])
    s0_d = din("s0", [2, 4, 128, 128]); condT_d = din("condT", [128, 16]); alpha_d = din("alpha", [128, 3])
    wfs_d = din("wfs", [3, 1024, 512]); lbs_d = din("lbs", [1, 3 * 2 * 512])
    cst_d = din("cst", [128, NCST])
    w_ada_d = din("w_ada", [1024, 6144]); b_adaT_d = din("b_adaT", [128, 48])
    n1T_d = din("n1T", [128, 8]); n2T_d = din("n2T", [128, 8])
    w_in_d = din("w_in", [1024, 5632]); sguT_d = din("sguT", [128, 4]); w_sp_d = din("w_sp", [4, 128, 128])
    b_sp_d = din("b_sp", [1, 512]); lb_d = din("lb", [1, 2048]); hgT_d = din("hgT", [128, 4])
    wpa_d = din("wpa", [512, 1024]); wpb_d = din("wpb", [512, 1024]); wo_d = din("wo", [1024, 1024])
    wq_d = din("wq", [1024, 2048]); sk_d = din("sk", [16, 128, 128])
    pu_d = din("pu", [128, 128, 1024]); pv_d = din("pv", [16384, 1024]); fg_d = din("fg", [1, 1024])

    yp_d = dout("yp", [512, 1024]); ys_d = dout("ys", [512, 1024]); st_d = dout("st", [2, 2, 4, 128, 128])
    xres_d = nc.dram_tensor("xres", [1024, 1024], F32, kind="Internal").ap()
    x1s_d = nc.dram_tensor("x1s", [1024, 1024], F32, kind="Internal").ap()
    h2s_d = nc.dram_tensor("h2s", [1024, 1024], BF16, kind="Internal").ap()
    gd_d = nc.dram_tensor("gd", [128, 128, 1024], BF16, kind="Internal").ap()
    gls_d = nc.dram_tensor("gls", [128, 128, 1024], BF16, kind="Internal").ap()

    dbg_list = []

    with ExitStack() as gst:
        S = Sched(nc, gst)

        def mk(stack):
            def T(name, shape, dt=F32):
                return stack.enter_context(nc.sbuf_tensor("sb_" + name, list(shape), dt))
            return T
        GT = mk(gst)

        PS = {n: gst.enter_context(nc.psum_tensor("ps_" + n, [128, 512], F32)) for n in "abcdefg"}
        PT = gst.enter_context(nc.psum_tensor("ps_t", [128, 1024], BF16))

        def dbg(name, ap, shape, reads, dt=F32):
            if not debug:
                return
            d = dout("dbg_" + name, shape, dt)
            DEBUG_OUTS[name] = (tuple(shape), dt)
            S.dma("sync", d, ap, reads=reads)

        _breg = {}

        def breg(e):
            if "r" not in _breg:
                _breg["r"] = e.to_reg(16383)
            return _breg["r"]

        V = lambda fn, r=(), w=(): S.op("vector", fn, r, w)
        A = lambda fn, r=(), w=(): S.op("scalar", fn, r, w)
        G = lambda fn, r=(), w=(): S.op("gpsimd", fn, r, w)
        P = lambda fn, r=(), w=(): S.op("tensor", fn, r, w)

        cst = GT("cst", [128, NCST])
        S.dma("sync", cst[:], cst_d, writes=["cst"])

        def C(name):
            o, n = CST_LAYOUT[name]
            return cst[:, o:o + n]

        ident_bf = GT("ident_bf", [128, 128], BF16)
        ones_bf = GT("ones_bf", [128, 128], BF16)
        V(lambda e: e.tensor_copy(out=ident_bf[:], in_=C("ident")), ["cst"], ["ident_bf"])
        V(lambda e: e.tensor_copy(out=ones_bf[:], in_=C("ones")), ["cst"], ["ones_bf"])

        modT = GT("modT", [128, 48, 2])
        sc = GT("sc", [128, 8, 2], BF16); b_adaT = GT("b_adaT", [128, 48])
        G1T = GT("G1T", [128, 8, 2]); G2T = GT("G2T", [128, 8, 2])
        n1T = GT("n1T", [128, 8]); n2T = GT("n2T", [128, 8])
        sguT = GT("sguT", [128, 4]); hgT = GT("hgT", [128, 4])
        bsb = GT("bsb", [128, 512])
        wsT = GT("wsT", [128, 4, 128], BF16)
        dg = GT("dg", [128, 8, 128])
        alpha = GT("alpha", [128, 3]); oma = GT("oma", [128, 3])
        junkb = GT("junkb", [128, 1024], BF16)
        stat = GT("stat", [128, 8])
        tmpf = GT("tmpf", [128, 1024])
        h2T = GT("h2T", [128, 8, 8, 128], BF16)
        L1 = gst.enter_context(ExitStack()); T_L1 = mk(L1)
        hT_own = T_L1("hT_own", [128, 8, 8, 128], BF16)
        ybT = T_L1("ybT", [128, 8, 4, 128], BF16)
        L2 = L1.enter_context(ExitStack()); T_L2 = mk(L2)
        lb_b = T_L2("lb_b", [128, 2, 512]); omlb_b = T_L2("omlb_b", [128, 2, 512])
        G1b = T_L2("G1b", [128, 2, 1024], BF16); sh1b = T_L2("sh1b", [128, 2, 1024], BF16)
        Sst = T_L2("Sst", [128, 3, 2, 4, 128])
        tab = T_L2("tab", [128, 512])
        xt = [T_L2("xt%d" % i, [128, 1024]) for i in range(2)]
        hb = T_L2("hb", [128, 1024], BF16)

        for (dst, src, nm) in [(n1T, n1T_d, "n1T"), (n2T, n2T_d, "n2T"), (sguT, sguT_d, "sguT"), (hgT, hgT_d, "hgT"),
                               (alpha, alpha_d, "alpha")]:
            S.dma("sync", dst[:], src, writes=[nm])
        S.dma("sync", bsb[:], b_sp_d.partition_broadcast(128), writes=["bsb"])
        V(lambda e: e.tensor_scalar(out=oma[:], in0=alpha[:], scalar1=-1.0, scalar2=1.0, op0=ALU.mult, op1=ALU.add), ["alpha"], ["oma"])
        for sq in range(2):
            G(lambda e, sq=sq: e.memset(Sst[:, sq, :, :, :], 0.0), [], ["S%d0" % sq, "S%d1" % sq])
        for d in range(2):
            S.dma("sync", Sst[:, 2, d, :, :], s0_d[d].rearrange("h k v -> k h v"), writes=["S2%d" % d])

        def bcast_tile(dst_ap, val_ap, val_keys, dst_key):
            V(lambda e: e.tensor_tensor(out=dg[:], in0=C("ident").unsqueeze(1).to_broadcast([128, 8, 128]),
                                        in1=val_ap.unsqueeze(2).to_broadcast([128, 8, 128]), op=ALU.mult), ["cst"] + val_keys, ["dg"])
            for k in range(8):
                bank = "b" if k < 4 else "c"
                P(lambda e, k=k, bank=bank: e.matmul(PS[bank][:, (k % 4) * 128:(k % 4 + 1) * 128], lhsT=C("ones"), rhs=dg[:, k, :],
                                                     start=True, stop=True), ["cst", "dg"], ["p" + bank])
            A(lambda e: e.activation(out=dst_ap[:, 0:512], in_=PS["b"][:], func=AF.Copy), ["pb"], [dst_key + "lo"])
            A(lambda e: e.activation(out=dst_ap[:, 512:1024], in_=PS["c"][:], func=AF.Copy), ["pc"], [dst_key + "hi"])

        st1 = ExitStack()
        T1 = mk(st1)
        wfs = T1("wfs", [128, 3, 8, 512], BF16)
        wi = T1("wi", [128, 8, 512], BF16)
        w_in_v = w_in_d.rearrange("(k p) c -> p k c", p=128)
        with ExitStack() as st0:
            T0 = mk(st0)
            condT = T0("condT", [128, 16])
            wa = [T0("wa%d" % i, [128, 8, 512], BF16) for i in range(2)]
            lbr = T0("lbr", [128, 2048])
            wsp = T0("wsp", [128, 4, 128])
            t512 = [T0("t512_%d" % i, [128, 512]) for i in range(3)]
            ti512 = T0("ti512", [128, 512], I32)

            S.dma("sync", condT[:], condT_d, writes=["condT"])
            S.dma("sync", b_adaT[:], b_adaT_d, writes=["b_adaT"])
            S.dma("sync", lbr[:], lb_d.partition_broadcast(128), writes=["lbr"])
            S.dma("sync", wsp[:], w_sp_d.rearrange("g t s -> t g s"), writes=["wsp"])

            A(lambda e: e.activation(out=sc[:].rearrange("p k j -> p (k j)"), in_=condT[:], func=AF.Silu), ["condT"], ["sc"])
            w_ada_v = w_ada_d.rearrange("(k p) c -> p k c", p=128)
            for blk in range(4):
                wb = wa[blk % 2]
                S.dma("gpsimd", wb[:], w_ada_v[:, :, blk * 512:(blk + 1) * 512], writes=["wa%d" % (blk % 2)])
                for cc in range(4):
                    col = (blk * 4 + cc) * 2
                    for k in range(8):
                        P(lambda e, wb=wb, cc=cc, k=k, col=col: e.matmul(PS["a"][:, col:col + 2], lhsT=wb[:, k, cc * 128:(cc + 1) * 128],
                                                                         rhs=sc[:, k, :], start=(k == 0), stop=(k == 7)),
                          ["wa%d" % (blk % 2), "sc"], ["pa"])
            for s_ in range(3):
                S.dma("gpsimd", wfs[:, s_, :, :], wfs_d[s_].rearrange("(k p) c -> p k c", p=128), writes=["wfs"])
            S.dma("gpsimd", wi[:], w_in_v[:, :, 2560:3072], writes=["wi"])
            V(lambda e: e.tensor_tensor(out=modT[:, 0:16, :], in0=PS["a"][:, 0:32].rearrange("p (c j) -> p c j", j=2),
                                        in1=b_adaT[:, 0:16].unsqueeze(2).to_broadcast([128, 16, 2]), op=ALU.add), ["pa", "b_adaT"], ["modT"])
            for (GT_, nT, m, nm, nnm) in [(G1T, n1T, 1, "G1T", "n1T")]:
                V(lambda e, GT_=GT_, nT=nT, m=m: e.scalar_tensor_tensor(out=GT_[:], in0=modT[:, m * 8:(m + 1) * 8, :], scalar=1.0,
                                                                      in1=nT[:].unsqueeze(2).to_broadcast([128, 8, 2]), op0=ALU.add, op1=ALU.mult),
                  ["modT", nnm], [nm])

            for j in range(2):
                bcast_tile(G1b[:, j, :], G1T[:, :, j], ["G1T"], "G1b%d" % j)
                bcast_tile(sh1b[:, j, :], modT[:, 0:8, j], ["modT"], "sh1b%d" % j)

            V(lambda e: e.tensor_tensor(out=lb_b[:].rearrange("p d c -> p (d c)"), in0=lbr[:, 0:1024], in1=lbr[:, 1024:2048], op=ALU.subtract),
              ["lbr"], ["lb_b"])
            A(lambda e: e.activation(out=lb_b[:], in_=lb_b[:], func=AF.Sigmoid), ["lb_b"], ["lb_b"])
            V(lambda e: e.tensor_scalar(out=omlb_b[:], in0=lb_b[:], scalar1=-1.0, scalar2=1.0, op0=ALU.mult, op1=ALU.add), ["lb_b"], ["omlb_b"])

            for g in range(4):
                P(lambda e, g=g: e.transpose(out=PS["d"][:, g * 128:(g + 1) * 128], in_=wsp[:, g, :], identity=C("ident")), ["wsp", "cst"], ["pd"])
            A(lambda e: e.activation(out=wsT[:].rearrange("p g t -> p (g t)"), in_=PS["d"][:], func=AF.Copy), ["pd"], ["wsT"])
            om = t512[0]
            A(lambda e: e.activation(out=om[:, 0:256], in_=C("freqidx"), func=AF.Exp, scale=-math.log(10000.0) / 256.0), ["cst"], ["om"])
            arg = t512[1]
            V(lambda e: e.tensor_scalar(out=arg[:, 0:256], in0=om[:, 0:256], scalar1=C("pcol"), scalar2=None, op0=ALU.mult), ["om", "cst"], ["arg"])
            V(lambda e: e.tensor_scalar(out=arg[:, 256:512], in0=arg[:, 0:256], scalar1=math.pi / 2, scalar2=None, op0=ALU.add), ["arg"], ["arg"])
            tt = t512[2]
            V(lambda e: e.tensor_scalar(out=tt[:], in0=arg[:], scalar1=1.0 / (2 * math.pi), scalar2=0.5, op0=ALU.mult, op1=ALU.add), ["arg"], ["tt"])
            V(lambda e: e.tensor_copy(out=ti512[:], in_=tt[:]), ["tt"], ["ti"])
            V(lambda e: e.tensor_copy(out=om[:], in_=ti512[:]), ["ti"], ["om"])
            V(lambda e: e.tensor_tensor(out=tt[:], in0=tt[:], in1=om[:], op=ALU.subtract), ["tt", "om"], ["tt"])
            V(lambda e: e.tensor_scalar(out=om[:], in0=tt[:], scalar1=0.0, scalar2=None, op0=ALU.is_lt), ["tt"], ["om"])
            V(lambda e: e.tensor_tensor(out=tt[:], in0=tt[:], in1=om[:], op=ALU.add), ["tt", "om"], ["tt"])
            V(lambda e: e.tensor_scalar(out=tt[:], in0=tt[:], scalar1=2 * math.pi, scalar2=-math.pi, op0=ALU.mult, op1=ALU.add), ["tt"], ["tt"])
            V(lambda e: e.tensor_scalar(out=tt[:], in0=tt[:], scalar1=3.14159, scalar2=-3.14159, op0=ALU.min, op1=ALU.max), ["tt"], ["tt"])
            A(lambda e: e.activation(out=tab[:], in_=tt[:], func=AF.Sin), ["tt"], ["tab"])
            S.barrier()

        fe_cnt = [0]

        def front(x_src, j, sel_idx, hT_dst, hT_key, res_row=None, selbuf=None):
            i = fe_cnt[0] % 2
            fe_cnt[0] += 1
            x = xt[i]
            xk = "xt%d" % i
            S.dma("sync", x[:], x_src, writes=[xk])
            if sel_idx is not None:
                sb = selbuf[i]
                S.dma("sync", sb[:], sel_d[sel_idx].rearrange("a p n -> p a n"), writes=["selb%d" % i])
                P(lambda e: e.matmul(PS["a"][:], lhsT=sb[:, 0, :], rhs=tab[:], start=True, stop=True), ["selb%d" % i, "tab"], ["pa"])
                P(lambda e: e.matmul(PS["b"][:], lhsT=sb[:, 1, :], rhs=tab[:], start=True, stop=True), ["selb%d" % i, "tab"], ["pb"])
                V(lambda e: e.tensor_tensor(out=x[:, 0:512], in0=x[:, 0:512], in1=PS["a"][:], op=ALU.add), [xk, "pa"], [xk])
                V(lambda e: e.tensor_tensor(out=x[:, 512:1024], in0=x[:, 512:1024], in1=PS["b"][:], op=ALU.add), [xk, "pb"], [xk])
            if res_row is not None:
                S.dma("sync", xres_d[res_row:res_row + 128, :], x[:], reads=[xk], writes=["xres%d" % res_row])
            A(lambda e: e.activation(out=junkb[:], in_=x[:], func=AF.Square, accum_out=stat[:, 0:1]), [xk], ["junkb", "stat0"])
            A(lambda e: e.activation(out=stat[:, 1:2], in_=stat[:, 0:1], func=AF.Ln, scale=1.0 / 1024.0, bias=EPS), ["stat0"], ["stat1"])
            A(lambda e: e.activation(out=stat[:, 2:3], in_=stat[:, 1:2], func=AF.Exp, scale=-0.5), ["stat1"], ["stat2"])
            V(lambda e: e.scalar_tensor_tensor(out=tmpf[:], in0=x[:], scalar=stat[:, 2:3], in1=G1b[:, j, :], op0=ALU.mult, op1=ALU.mult),
              [xk, "stat2"], ["tmpf"])
            V(lambda e: e.tensor_tensor(out=hb[:], in0=tmpf[:], in1=sh1b[:, j, :], op=ALU.add), ["tmpf"], ["hb"])
            for k in range(8):
                P(lambda e, k=k: e.transpose(out=PT[:, k * 128:(k + 1) * 128], in_=hb[:, k * 128:(k + 1) * 128], identity=ident_bf[:]),
                  ["hb", "ident_bf"], ["pt"])
            A(lambda e: e.activation(out=hT_dst.rearrange("p k t -> p (k t)"), in_=PT[:], func=AF.Copy), ["pt"], [hT_key])

        def proj_tm(ps_ap, ps_key, hT, hT_key, w, w_key, c0, ncols):
            for k in range(8):
                P(lambda e, k=k: e.matmul(ps_ap, lhsT=hT[:, k, :], rhs=w[:, k, c0:c0 + ncols], start=(k == 0), stop=(k == 7)),
                  [hT_key, w_key], [ps_key])

        def proj_fm(ps_ap, ps_key, hT, hT_key, w, w_key, c0):
            for k in range(8):
                P(lambda e, k=k: e.matmul(ps_ap, lhsT=w[:, k, c0:c0 + 128], rhs=hT[:, k, :], start=(k == 0), stop=(k == 7)),
                  [hT_key, w_key], [ps_key])

        w_in_v = w_in_d.rearrange("(k p) c -> p k c", p=128)

        with st1:
            selbuf = [T1("selb%d" % i, [128, 2, 128]) for i in range(2)]
            lbs_r = T1("lbs_r", [128, 3, 2, 512])
            lb_s = T1("lb_s", [128, 3, 512]); omlb_s = T1("omlb_s", [128, 3, 512])
            hTs = [T1("hTs%d" % i, [128, 8, 128], BF16) for i in range(2)]
            lf_s = T1("lf_s", [128, 4, 512]); kk_s = T1("kk_s", [128, 4, 512])
            v_s = T1("v_s", [128, 4, 512], BF16); kd_s = T1("kd_s", [128, 4, 512], BF16)
            fs = T1("fs", [128, 512]); eE = T1("eE", [128, 512])
            A_s = T1("A_s", [128, 4]); Am1 = T1("Am1", [128, 4]); Ab = T1("Ab", [128, 2, 4])
            tS = T1("tS", [128, 512])
            wa2 = [T1("wa2_%d" % i, [128, 8, 256], BF16) for i in range(2)]
            w_ada_v2 = w_ada_d.rearrange("(k p) c -> p k c", p=128)

            def ada_sub(sb):
                wb = wa2[sb % 2]
                c0 = 2048 + sb * 256
                S.dma("gpsimd", wb[:], w_ada_v2[:, :, c0:c0 + 256], writes=["wa2_%d" % (sb % 2)])
                for cc in range(2):
                    col = 64 + (sb * 2 + cc) * 2
                    for k in range(8):
                        P(lambda e: e.matmul(PS["f"][:, col:col + 2], lhsT=wb[:, k, cc * 128:(cc + 1) * 128], rhs=sc[:, k, :],
                                             start=(k == 0), stop=(k == 7)), ["wa2_%d" % (sb % 2), "sc"], ["pf2"])

            S.dma("sync", lbs_r[:].rearrange("p s l c -> p (s l c)"), lbs_d.partition_broadcast(128), writes=["lbs_r"])
            V(lambda e: e.tensor_tensor(out=lb_s[:], in0=lbs_r[:, :, 0, :], in1=lbs_r[:, :, 1, :], op=ALU.subtract), ["lbs_r"], ["lb_s"])
            A(lambda e: e.activation(out=lb_s[:], in_=lb_s[:], func=AF.Sigmoid), ["lb_s"], ["lb_s"])
            V(lambda e: e.tensor_scalar(out=omlb_s[:], in0=lb_s[:], scalar1=-1.0, scalar2=1.0, op0=ALU.mult, op1=ALU.add), ["lb_s"], ["omlb_s"])

            for t in range(8):
                if t < 4:
                    front(xp_d[t * 128:(t + 1) * 128, :], 0, None, hT_own[:, t, :, :], "hT%d" % t, res_row=t * 128)
                else:
                    front(xs_d[(t - 4) * 128:(t - 3) * 128, :], 1, t - 4, hT_own[:, t, :, :], "hT%d" % t, res_row=t * 128, selbuf=selbuf)
                if t % 2 == 1:
                    ada_sub(t // 2)
            if debug:
                dbg("hT0", hT_own[:, 0, :, :], [128, 8, 128], ["hT0"], BF16)
                dbg("hT4", hT_own[:, 4, :, :], [128, 8, 128], ["hT4"], BF16)

            stiles = [(s_, j_) for s_ in range(3) for j_ in range(4)]

            def slot_front(idx):
                s_, j_ = stiles[idx]
                front(xo_d[s_, j_ * 128:(j_ + 1) * 128, :], 1, 4 + s_ * 4 + j_, hTs[idx % 2][:], "hTs%d" % (idx % 2), selbuf=selbuf)

            slot_front(0)
            for sidx, (s, jj) in enumerate(stiles):
                if True:
                    if sidx + 1 < 12:
                        slot_front(sidx + 1)
                    ada_sub(4 + sidx)
                    hh = hTs[sidx % 2]
                    hk = "hTs%d" % (sidx % 2)
                    proj_tm(PS["c"][:], "pc", hh, hk, wfs[:, s, :, :], "wfs", 0, 512)
                    proj_tm(PS["d"][:], "pd", hh, hk, wi, "wi", 0, 512)
                    A(lambda e: e.activation(out=fs[:], in_=PS["c"][:], func=AF.Exp, scale=-1.0), ["pc"], ["fs"])
                    A(lambda e: e.activation(out=fs[:], in_=fs[:], func=AF.Ln, bias=1.0), ["fs"], ["fs"])
                    A(lambda e: e.activation(out=fs[:], in_=fs[:], func=AF.Exp, scale=-1.0), ["fs"], ["fs"])
                    A(lambda e, jj=jj: e.activation(out=v_s[:, jj, :], in_=PS["d"][:], func=AF.Copy), ["pd"], ["v_s%d" % jj])
                    V(lambda e, s=s: e.tensor_tensor(out=fs[:], in0=fs[:], in1=omlb_s[:, s, :], op=ALU.mult), ["fs", "omlb_s"], ["fs"])
                    V(lambda e, s=s: e.tensor_tensor(out=fs[:], in0=fs[:], in1=lb_s[:, s, :], op=ALU.add), ["fs", "lb_s"], ["fs"])
                    A(lambda e, jj=jj: e.activation(out=lf_s[:, jj, :], in_=fs[:], func=AF.Ln), ["fs"], ["lf_s%d" % jj])
                    V(lambda e, jj=jj: e.tensor_scalar(out=kk_s[:, jj, :], in0=fs[:], scalar1=-1.0, scalar2=1.0, op0=ALU.mult, op1=ALU.add),
                      ["fs"], ["kk_s%d" % jj])
                if jj != 3:
                    continue
                for jj in range(4):
                    P(lambda e, jj=jj: e.matmul(PS["e"][:], lhsT=C("tris"), rhs=lf_s[:, jj, :], start=True, stop=(jj == 3)),
                      ["cst", "lf_s%d" % jj], ["pe"])
                    for j2 in range(jj + 1, 4):
                        P(lambda e, j2=j2: e.matmul(PS["e"][:], lhsT=C("ones"), rhs=lf_s[:, j2, :], start=False, stop=(j2 == 3)),
                          ["cst", "lf_s%d" % j2], ["pe"])
                    A(lambda e: e.activation(out=eE[:], in_=PS["e"][:], func=AF.Exp), ["pe"], ["eE"])
                    V(lambda e, jj=jj: e.tensor_tensor(out=kd_s[:, jj, :], in0=kk_s[:, jj, :], in1=eE[:], op=ALU.mult),
                      ["eE", "kk_s%d" % jj], ["kd_s%d" % jj])
                for h in range(4):
                    for jj in range(4):
                        P(lambda e, h=h, jj=jj: e.matmul(PS["f"][:, 2 * h:2 * h + 2], lhsT=lf_s[:, jj, h * 128:(h + 1) * 128], rhs=C("ones")[:, 0:2],
                                                         start=(jj == 0), stop=(jj == 3)), ["cst", "lf_s%d" % jj], ["pf"])
                for h in range(4):
                    for jj in range(4):
                        P(lambda e, h=h, jj=jj: e.matmul(PS["g"][:, h * 128:(h + 1) * 128], lhsT=kd_s[:, jj, h * 128:(h + 1) * 128],
                                                         rhs=v_s[:, jj, h * 128:(h + 1) * 128], start=(jj == 0), stop=(jj == 3)),
                          ["kd_s%d" % jj, "v_s%d" % jj], ["pg"])
                A(lambda e: e.activation(out=A_s[:], in_=PS["f"][:, 0:8].rearrange("p (h two) -> p h two", two=2)[:, :, 0], func=AF.Exp), ["pf"], ["A_s"])
                V(lambda e: e.tensor_scalar(out=Am1[:], in0=A_s[:], scalar1=-1.0, scalar2=None, op0=ALU.add), ["A_s"], ["Am1"])
                for d, acol in enumerate([alpha, oma]):
                    V(lambda e, d=d, acol=acol, s=s: e.tensor_scalar(out=Ab[:, d, :], in0=Am1[:], scalar1=acol[:, s:s + 1], scalar2=1.0,
                                                                    op0=ALU.mult, op1=ALU.add), ["Am1", "alpha", "oma"], ["Ab%d" % d])
                    V(lambda e, acol=acol, s=s: e.tensor_scalar(out=tS[:], in0=PS["g"][:], scalar1=acol[:, s:s + 1], scalar2=None, op0=ALU.mult),
                      ["pg", "alpha", "oma"], ["tS"])
                    for h in range(4):
                        V(lambda e, d=d, h=h: e.scalar_tensor_tensor(out=Sst[:, 2, d, h, :], in0=Sst[:, 2, d, h, :], scalar=Ab[:, d, h:h + 1],
                                                                     in1=tS[:, h * 128:(h + 1) * 128], op0=ALU.mult, op1=ALU.add),
                          ["S2%d" % d, "Ab%d" % d, "tS"], ["S2%d" % d])
            if debug:
                dbg("S2", Sst[:, 2, :, :, :], [128, 2, 4, 128], ["S20", "S21"])
            V(lambda e: e.tensor_tensor(out=modT[:, 16:48, :], in0=PS["f"][:, 64:128].rearrange("p (c j) -> p c j", j=2),
                                        in1=b_adaT[:, 16:48].unsqueeze(2).to_broadcast([128, 32, 2]), op=ALU.add), ["pf2", "b_adaT"], ["modT2"])
            V(lambda e: e.scalar_tensor_tensor(out=G2T[:], in0=modT[:, 32:40, :], scalar=1.0, in1=n2T[:].unsqueeze(2).to_broadcast([128, 8, 2]),
                                               op0=ALU.add, op1=ALU.mult), ["modT2", "n2T"], ["G2T"])
            S.barrier()

        seq_tiles = [[0, 1], [2, 3], [4, 5, 6, 7]]
        tile_seq = {t: sq for sq, ts in enumerate(seq_tiles) for t in ts}
        with ExitStack() as st2:
            T2 = mk(st2)
            whg = T2("whg", [128, 8, 2560], BF16)
            zq_sb = T2("zq_sb", [128, 8, 512], BF16)
            v_sb = T2("v_sb", [128, 8, 512], BF16)
            ofw = T2("ofw", [128, 8, 512], BF16)
            f_ = T2("f_", [128, 512]); lf = T2("lf", [128, 512]); kk = T2("kk", [128, 512])
            eb = T2("eb", [128, 512]); enb = T2("enb", [128, 512]); ee2 = T2("ee2", [128, 512])
            qt = T2("qt", [128, 512], BF16); kt = T2("kt", [128, 512], BF16); kd = T2("kd", [128, 512], BF16)
            kdm = T2("kdm", [128, 4, 512], BF16)
            qkT = T2("qkT", [128, 8, 128], BF16)
            scm = T2("scm", [128, 4, 128], BF16)
            A_sb = T2("A_sb", [128, 16])
            Sbf = T2("Sbf", [128, 4, 4, 128], BF16)
            osq = T2("osq", [128, 512], BF16)

            for c0 in (0, 512, 1536, 1024, 2048):
                S.dma("gpsimd", whg[:, :, c0:c0 + 512], w_in_v[:, :, 1024 + c0:1024 + c0 + 512], writes=["whg%d" % (c0 // 512)])

            def hgrn_a(t, d):
                hT = hT_own[:, t, :, :]
                hk = "hT%d" % t
                if d == 0:
                    proj_tm(PS["e"][:], "pe", hT, hk, whg, "whg0", 0, 512)
                    A(lambda e: e.activation(out=zq_sb[:, t, :], in_=PS["e"][:], func=AF.Copy), ["pe"], ["zq%d" % t])
                    proj_tm(PS["e"][:], "pe", hT, hk, whg, "whg3", 1536, 512)
                    A(lambda e: e.activation(out=v_sb[:, t, :], in_=PS["e"][:], func=AF.Copy), ["pe"], ["v%d" % t])
                proj_tm(PS["d"][:], "pd", hT, hk, whg, "whg%d" % (1 + d), 512 + 512 * d, 512)
                A(lambda e: e.activation(out=f_[:], in_=PS["d"][:], func=AF.Exp, scale=-1.0), ["pd"], ["f_"])
                A(lambda e: e.activation(out=f_[:], in_=f_[:], func=AF.Ln, bias=1.0), ["f_"], ["f_"])
                A(lambda e: e.activation(out=f_[:], in_=f_[:], func=AF.Exp, scale=-1.0), ["f_"], ["f_"])
                V(lambda e: e.tensor_tensor(out=f_[:], in0=f_[:], in1=omlb_b[:, d, :], op=ALU.mult), ["f_"], ["f_"])
                V(lambda e: e.tensor_tensor(out=f_[:], in0=f_[:], in1=lb_b[:, d, :], op=ALU.add), ["f_"], ["f_"])
                A(lambda e: e.activation(out=lf[:], in_=f_[:], func=AF.Ln), ["f_"], ["lf"])
                G(lambda e: e.tensor_scalar(out=kk[:], in0=f_[:], scalar1=-1.0, scalar2=1.0, op0=ALU.mult, op1=ALU.add), ["f_"], ["kk"])

            def hgrn_b(t, d, nxt):
                sq = tile_seq[t]
                hT = hT_own[:, t, :, :]
                hk = "hT%d" % t
                skey = "S%d%d" % (sq, d)
                m1 = C("mfw") if d == 0 else C("mbw")
                m2 = C("m2fw") if d == 0 else C("m2bw")
                P(lambda e: e.matmul(PS["b"][:], lhsT=m1, rhs=lf[:], start=True, stop=True), ["cst", "lf"], ["pb"])
                P(lambda e: e.matmul(PS["c"][:], lhsT=m2, rhs=lf[:], start=True, stop=True), ["cst", "lf"], ["pc"])
                for h in range(4):
                    P(lambda e, h=h: e.matmul(PS["d"][:, h * 4:(h + 1) * 4], lhsT=lf[:, h * 128:(h + 1) * 128], rhs=C("ci"), start=True, stop=True),
                      ["cst", "lf"], ["pd"])
                A(lambda e: e.activation(out=eb[:], in_=PS["b"][:], func=AF.Exp), ["pb"], ["eb"])
                A(lambda e: e.activation(out=enb[:], in_=PS["b"][:], func=AF.Exp, scale=-1.0), ["pb"], ["enb"])
                A(lambda e: e.activation(out=ee2[:], in_=PS["c"][:], func=AF.Exp), ["pc"], ["ee2"])
                A(lambda e: e.activation(out=A_sb[:], in_=PS["d"][:, 0:16], func=AF.Exp), ["pd"], ["A_sb"])
                V(lambda e: e.tensor_tensor(out=qt[:], in0=zq_sb[:, t, :], in1=eb[:], op=ALU.mult), ["zq%d" % t, "eb"], ["qt"])
                V(lambda e: e.tensor_tensor(out=kt[:], in0=kk[:], in1=enb[:], op=ALU.mult), ["kk", "enb"], ["kt"])
                V(lambda e: e.tensor_tensor(out=kd[:], in0=kk[:], in1=ee2[:], op=ALU.mult), ["kk", "ee2"], ["kd"])
                V(lambda e: e.tensor_tensor(out=kdm[:], in0=kd[:].unsqueeze(1).to_broadcast([128, 4, 512]),
                                            in1=C("ci").unsqueeze(2).to_broadcast([128, 4, 512]), op=ALU.mult), ["kd", "cst"], ["kdm"])
                for h in range(4):
                    P(lambda e, h=h: e.transpose(out=PT[:, h * 128:(h + 1) * 128], in_=qt[:, h * 128:(h + 1) * 128], identity=ident_bf[:]),
                      ["qt", "ident_bf"], ["pt"])
                    P(lambda e, h=h: e.transpose(out=PT[:, (4 + h) * 128:(5 + h) * 128], in_=kt[:, h * 128:(h + 1) * 128], identity=ident_bf[:]),
                      ["kt", "ident_bf"], ["pt"])
                A(lambda e: e.activation(out=qkT[:].rearrange("p k t -> p (k t)"), in_=PT[:], func=AF.Copy), ["pt"], ["qkT"])
                for h in range(4):
                    P(lambda e, h=h: e.matmul(PS["e"][:, h * 128:(h + 1) * 128], lhsT=qkT[:, 4 + h, :], rhs=qkT[:, h, :], start=True, stop=True),
                      ["qkT"], ["pe"])
                V(lambda e: e.tensor_tensor(out=scm[:], in0=PS["e"][:].rearrange("p (h t) -> p h t", h=4),
                                            in1=m1.unsqueeze(1).to_broadcast([128, 4, 128]), op=ALU.mult), ["pe", "cst"], ["scm"])
                corder = [0, 1, 2, 3] if d == 0 else [3, 2, 1, 0]
                dbank = ["f", "g", "b", "c"]
                for h in range(4):
                    bank = dbank[h]
                    for c in range(4):
                        P(lambda e: e.matmul(PS[bank][:, c * 128:(c + 1) * 128], lhsT=kdm[:, c, h * 128:(h + 1) * 128],
                                             rhs=v_sb[:, t, h * 128:(h + 1) * 128], start=True, stop=True),
                          ["kdm", "v%d" % t], ["p" + bank])
                for c in corder:
                    for h in range(4):
                        bank = dbank[h]
                        hkey = skey + "_%d" % h
                        A(lambda e: e.activation(out=Sbf[:, h, c, :], in_=Sst[:, sq, d, h, :], func=AF.Copy), [hkey], ["Sbf%d_%d" % (h, c)])
                        V(lambda e: e.scalar_tensor_tensor(out=Sst[:, sq, d, h, :], in0=Sst[:, sq, d, h, :], scalar=A_sb[:, h * 4 + c:h * 4 + c + 1],
                                                           in1=PS[bank][:, c * 128:(c + 1) * 128], op0=ALU.mult, op1=ALU.add),
                          [hkey, "A_sb", "p" + bank], [hkey])
                if nxt is not None:
                    hgrn_a(*nxt)
                for h in range(4):
                    P(lambda e, h=h: e.matmul(PS["a"][:, h * 128:(h + 1) * 128], lhsT=v_sb[:, t, h * 128:(h + 1) * 128], rhs=scm[:, h, :],
                                              start=True, stop=False), ["v%d" % t, "scm"], ["pa"])
                    for c in range(4):
                        P(lambda e, h=h, c=c: e.matmul(PS["a"][:, h * 128 + c * 32:h * 128 + (c + 1) * 32], lhsT=Sbf[:, h, c, :],
                                                       rhs=qkT[:, h, c * 32:(c + 1) * 32], start=False, stop=(c == 3)),
                          ["Sbf%d_%d" % (h, c), "qkT"], ["pa"])
                if d == 0:
                    A(lambda e: e.activation(out=ofw[:, t, :], in_=PS["a"][:], func=AF.Copy), ["pa"], ["ofw%d" % t])
                else:
                    V(lambda e: e.tensor_tensor(out=eb[:], in0=PS["a"][:], in1=ofw[:, t, :], op=ALU.add), ["pa", "ofw%d" % t], ["eb"])
                    A(lambda e: e.activation(out=osq[:], in_=eb[:], func=AF.Square), ["eb"], ["osq"])
                    P(lambda e: e.matmul(PS["c"][:], lhsT=ones_bf[:], rhs=osq[:], start=True, stop=True), ["osq", "ones_bf"], ["pc"])
                    A(lambda e: e.activation(out=enb[:], in_=PS["c"][:], func=AF.Ln, scale=1.0 / 128.0, bias=EPS), ["pc"], ["enb"])
                    A(lambda e: e.activation(out=enb[:], in_=enb[:], func=AF.Exp, scale=-0.5), ["enb"], ["enb"])
                    for h in range(4):
                        proj_fm(PS["d"][:, h * 128:(h + 1) * 128], "pd", hT, hk, whg, "whg4", 2048 + h * 128)
                    A(lambda e: e.activation(out=ee2[:], in_=PS["d"][:], func=AF.Exp, scale=-1.0), ["pd"], ["ee2"])
                    A(lambda e: e.activation(out=ee2[:], in_=ee2[:], func=AF.Ln, bias=1.0), ["ee2"], ["ee2"])
                    A(lambda e: e.activation(out=ee2[:], in_=ee2[:], func=AF.Exp, scale=-1.0), ["ee2"], ["ee2"])
                    V(lambda e: e.tensor_tensor(out=ee2[:], in0=PS["d"][:], in1=ee2[:], op=ALU.mult), ["pd", "ee2"], ["ee2"])
                    V(lambda e: e.tensor_tensor(out=eb[:], in0=eb[:], in1=enb[:], op=ALU.mult), ["eb", "enb"], ["eb"])
                    for h in range(4):
                        V(lambda e, h=h: e.scalar_tensor_tensor(out=ybT[:, t, h, :], in0=eb[:, h * 128:(h + 1) * 128], scalar=hgT[:, h:h + 1],
                                                                in1=ee2[:, h * 128:(h + 1) * 128], op0=ALU.mult, op1=ALU.mult),
                          ["eb", "ee2", "hgT"], ["yb%d" % t])

            steps = [(t, 0) for t in range(8)] + [(t, 1) for ts in seq_tiles for t in reversed(ts)]
            hgrn_a(*steps[0])
            for si, (t, d) in enumerate(steps):
                hgrn_b(t, d, steps[si + 1] if si + 1 < len(steps) else None)
                if si == 7:
                    for sq in (0, 1):
                        S.dma("sync", st_d[sq, 0].rearrange("h k v -> k h v"), Sst[:, sq, 0, :, :], reads=["S%d0_%d" % (sq, h) for h in range(4)])
            for sq in (0, 1):
                S.dma("sync", st_d[sq, 1].rearrange("h k v -> k h v"), Sst[:, sq, 1, :, :], reads=["S%d1_%d" % (sq, h) for h in range(4)])
            if debug:
                dbg("yb0", ybT[:, 0, :, :], [128, 4, 128], ["yb0"], BF16)
                dbg("yb5", ybT[:, 5, :, :], [128, 4, 128], ["yb5"], BF16)
            S.barrier()
        L2.close()

        with ExitStack() as st3:
            T3 = mk(st3)
            gate1b = T3("gate1b", [128, 2, 1024]); G2b = T3("G2b", [128, 2, 1024], BF16); sh2b = T3("sh2b", [128, 2, 1024], BF16)
            h2t = [T3("h2t%d" % i, [128, 1024], BF16) for i in range(2)]
            for j in range(2):
                bcast_tile(gate1b[:, j, :], modT[:, 16:24, j], ["modT"], "gate1b%d" % j)
                bcast_tile(G2b[:, j, :], G2T[:, :, j], ["G2T"], "G2b%d" % j)
                bcast_tile(sh2b[:, j, :], modT[:, 24:32, j], ["modT"], "sh2b%d" % j)
            wuv = T3("wuv", [128, 8, 1024], BF16)
            wab = T3("wab", [128, 8, 2048], BF16)
            wpa = T3("wpa", [128, 4, 1024], BF16); wpb = T3("wpb", [128, 4, 1024], BF16)
            wo = T3("wo", [128, 8, 1024], BF16)
            uT = T3("uT", [128, 512], BF16)
            gv = T3("gv", [128, 512]); vhat = T3("vhat", [128, 512], BF16)
            vs = T3("vs", [128, 512]); yaT = [T3("yaT%d" % i, [128, 4, 128], BF16) for i in range(2)]
            sa = [T3("sa%d" % i, [128, 1024]) for i in range(2)]; sbb = [T3("sbb%d" % i, [128, 1024]) for i in range(2)]
            junkx = T3("junkx", [128, 512], BF16)
            t1 = T3("t1", [128, 1024]); mixT = T3("mixT", [128, 8, 128], BF16); mixb = T3("mixb", [128, 1024], BF16)
            xr = T3("xr", [128, 1024]); x1 = xr

            for c0 in range(0, 1024, 512):
                S.dma("gpsimd", wuv[:, :, c0:c0 + 512], w_in_v[:, :, c0:c0 + 512], writes=["wuv"])
            for c0 in range(0, 2048, 512):
                S.dma("gpsimd", wab[:, :, c0:c0 + 512], w_in_v[:, :, 3584 + c0:3584 + c0 + 512], writes=["wab"])
            S.dma("gpsimd", wpa[:], wpa_d.rearrange("(k p) c -> p k c", p=128), writes=["wpa"])
            S.dma("gpsimd", wpb[:], wpb_d.rearrange("(k p) c -> p k c", p=128), writes=["wpb"])
            S.dma("gpsimd", wo[:], wo_d.rearrange("(k p) c -> p k c", p=128), writes=["wo"])

            def stage_x1(t):
                p = t % 2
                hT = hT_own[:, t, :, :]
                hk = "hT%d" % t
                for g in range(4):
                    proj_fm(PS["a"][:, g * 128:(g + 1) * 128], "pa", hT, hk, wuv, "wuv", g * 128)
                A(lambda e: e.activation(out=uT[:], in_=PS["a"][:], func=AF.Gelu_apprx_tanh), ["pa"], ["uT"])
                proj_tm(PS["b"][:], "pb", hT, hk, wuv, "wuv", 512, 512)
                A(lambda e: e.activation(out=gv[:], in_=PS["b"][:], func=AF.Gelu_apprx_tanh), ["pb"], ["gv"])
                for half, (dst, nm) in enumerate([(sa[p], "sa%d_" % p), (sbb[p], "sbb%d_" % p)]):
                    for q in range(2):
                        bank = "d" if q == 0 else "e"
                        proj_tm(PS[bank][:], "p" + bank, hT, hk, wab, "wab", half * 1024 + q * 512, 512)
                        A(lambda e: e.activation(out=dst[:, q * 512:(q + 1) * 512], in_=PS[bank][:], func=AF.Sigmoid), ["p" + bank], [nm + str(q)])

            def stage_x2(t):
                p = t % 2
                A(lambda e: e.activation(out=junkx[:], in_=gv[:], func=AF.Square, accum_out=stat[:, 3:4]), ["gv"], ["junkx", "stat3"])
                A(lambda e: e.activation(out=stat[:, 4:5], in_=stat[:, 3:4], func=AF.Ln, scale=1.0 / 512.0, bias=EPS), ["stat3"], ["stat4"])
                A(lambda e: e.activation(out=stat[:, 5:6], in_=stat[:, 4:5], func=AF.Exp, scale=-0.5), ["stat4"], ["stat5"])
                V(lambda e: e.tensor_scalar(out=vhat[:], in0=gv[:], scalar1=stat[:, 5:6], scalar2=None, op0=ALU.mult), ["gv", "stat5"], ["vhat"])
                for g in range(4):
                    P(lambda e: e.matmul(PS["c"][:, g * 128:(g + 1) * 128], lhsT=vhat[:, g * 128:(g + 1) * 128], rhs=wsT[:, g, :],
                                         start=True, stop=True), ["vhat", "wsT"], ["pc"])
                for g in range(4):
                    V(lambda e: e.scalar_tensor_tensor(out=vs[:, g * 128:(g + 1) * 128], in0=PS["c"][:, g * 128:(g + 1) * 128],
                                                       scalar=sguT[:, g:g + 1], in1=bsb[:, g * 128:(g + 1) * 128], op0=ALU.mult, op1=ALU.add),
                      ["pc", "sguT", "bsb"], ["vs"])
                V(lambda e: e.tensor_tensor(out=yaT[p][:].rearrange("p g t -> p (g t)"), in0=uT[:], in1=vs[:], op=ALU.mult), ["uT", "vs"], ["yaT%d" % p])

            def stage_y1(t):
                p = t % 2
                j = 0 if t < 4 else 1
                S.dma("sync", xr[:], xres_d[t * 128:(t + 1) * 128, :], reads=["xres%d" % (t * 128)], writes=["xr"])
                for q in range(2):
                    bank = "f" if q == 0 else "g"
                    for ac in range(4):
                        P(lambda e: e.matmul(PS[bank][:], lhsT=yaT[p][:, ac, :], rhs=wpa[:, ac, q * 512:(q + 1) * 512], start=(ac == 0), stop=(ac == 3)),
                          ["wpa", "yaT%d" % p], ["p" + bank])
                    V(lambda e: e.tensor_tensor(out=t1[:, q * 512:(q + 1) * 512], in0=PS[bank][:], in1=sa[p][:, q * 512:(q + 1) * 512], op=ALU.mult),
                      ["p" + bank, "sa%d_%d" % (p, q)], ["t1_%d" % q])
                for q in range(2):
                    bank = "f" if q == 0 else "g"
                    for ac in range(4):
                        P(lambda e: e.matmul(PS[bank][:], lhsT=ybT[:, t, ac, :], rhs=wpb[:, ac, q * 512:(q + 1) * 512], start=(ac == 0), stop=(ac == 3)),
                          ["wpb", "yb%d" % t], ["p" + bank])
                    V(lambda e: e.tensor_tensor(out=sbb[p][:, q * 512:(q + 1) * 512], in0=PS[bank][:], in1=sbb[p][:, q * 512:(q + 1) * 512], op=ALU.mult),
                      ["p" + bank, "sbb%d_%d" % (p, q)], ["sbb%d_%d" % (p, q)])
                    V(lambda e: e.tensor_tensor(out=mixb[:, q * 512:(q + 1) * 512], in0=t1[:, q * 512:(q + 1) * 512], in1=sbb[p][:, q * 512:(q + 1) * 512],
                                                op=ALU.add), ["t1_%d" % q, "sbb%d_%d" % (p, q)], ["mixb%d" % q])
                for k in range(8):
                    P(lambda e: e.transpose(out=PT[:, k * 128:(k + 1) * 128], in_=mixb[:, k * 128:(k + 1) * 128], identity=ident_bf[:]),
                      ["mixb%d" % (k // 4), "ident_bf"], ["pt"])
                A(lambda e: e.activation(out=mixT[:].rearrange("p k t -> p (k t)"), in_=PT[:], func=AF.Copy), ["pt"], ["mixT"])

            def stage_y2(t):
                p = t % 2
                j = 0 if t < 4 else 1
                for q in range(2):
                    bank = "f" if q == 0 else "g"
                    for k in range(8):
                        P(lambda e: e.matmul(PS[bank][:], lhsT=mixT[:, k, :], rhs=wo[:, k, q * 512:(q + 1) * 512], start=(k == 0), stop=(k == 7)),
                          ["mixT", "wo"], ["p" + bank])
                    V(lambda e: e.tensor_tensor(out=t1[:, q * 512:(q + 1) * 512], in0=PS[bank][:], in1=gate1b[:, j, q * 512:(q + 1) * 512], op=ALU.mult),
                      ["p" + bank], ["t1_%d" % q])
                V(lambda e: e.tensor_tensor(out=x1[:], in0=t1[:], in1=xr[:], op=ALU.add), ["t1_0", "t1_1", "xr"], ["xr"])
                S.dma("sync", x1s_d[t * 128:(t + 1) * 128, :], x1[:], reads=["xr"], writes=["x1s%d" % t])
                if debug and t in (0, 5):
                    dbg("x1_%d" % t, x1[:], [128, 1024], ["xr"])
                h2c = h2t[t % 2]
                hkey = "h2t%d" % (t % 2)
                A(lambda e: e.activation(out=junkb[:], in_=x1[:], func=AF.Square, accum_out=stat[:, 0:1]), ["xr"], ["junkb", "stat0"])
                A(lambda e: e.activation(out=stat[:, 1:2], in_=stat[:, 0:1], func=AF.Ln, scale=1.0 / 1024.0, bias=EPS), ["stat0"], ["stat1"])
                A(lambda e: e.activation(out=stat[:, 2:3], in_=stat[:, 1:2], func=AF.Exp, scale=-0.5), ["stat1"], ["stat2"])
                V(lambda e: e.scalar_tensor_tensor(out=tmpf[:], in0=x1[:], scalar=stat[:, 2:3], in1=G2b[:, j, :], op0=ALU.mult, op1=ALU.mult),
                  ["xr", "stat2"], ["tmpf"])
                V(lambda e: e.tensor_tensor(out=h2c[:], in0=tmpf[:], in1=sh2b[:, j, :], op=ALU.add), ["tmpf"], [hkey])

            def stage_y3(t):
                h2c = h2t[t % 2]
                hkey = "h2t%d" % (t % 2)
                for k in range(8):
                    P(lambda e: e.transpose(out=PT[:, k * 128:(k + 1) * 128], in_=h2c[:, k * 128:(k + 1) * 128], identity=ident_bf[:]),
                      [hkey, "ident_bf"], ["pt"])
                A(lambda e: e.activation(out=h2T[:, t, :, :].rearrange("p k t -> p (k t)"), in_=PT[:], func=AF.Copy), ["pt"], ["h2T%d" % t])

            stage_x1(0)
            stage_x2(0)
            for t in range(8):
                stage_y1(t)
                if t + 1 < 8:
                    stage_x1(t + 1)
                stage_y2(t)
                if t + 1 < 8:
                    stage_x2(t + 1)
                stage_y3(t)
            S.barrier()
        L1.close()

        with ExitStack() as st4:
            T4 = mk(st4)
            wq = T4("wq", [128, 8, 2048], BF16)
            qT_sb = T4("qT_sb", [128, 16, 128], BF16)
            sc2 = [T4("sc_sb%d" % i, [128, 16, 128]) for i in range(2)]
            v16 = T4("v16", [128, 16, 16]); i16u = T4("i16u", [128, 16, 16], U32); i16f = T4("i16f", [128, 16, 16])
            wk = T4("wk", [128, 16, 128])
            cand = T4("cand", [128, 8, 16, 16]); cwk = T4("cwk", [128, 8, 256])
            tv = T4("tv", [128, 8, 16]); ju = T4("ju", [128, 8, 16], U32)
            au = T4("au", [128, 8, 16], U32); bu = T4("bu", [128, 8, 16], U32)
            af = T4("af", [128, 8, 16]); bf = T4("bf", [128, 8, 16])
            eq = T4("eq", [128, 8, 16, 16])
            i1s = T4("i1s", [128, 8, 16]); i2s = T4("i2s", [128, 8, 16])
            ev = T4("ev", [128, 8, 16]); zs = T4("zs", [128, 8]); gg = T4("gg", [128, 8, 16])
            skT = T4("skT", [128, 16, 128], BF16); skl = T4("skl", [128, 16, 128])
            slT = T4("slT", [128, 384])
            oj = [T4("oj%d" % i, [128, 16, 128], BF16) for i in range(2)]
            oi = [T4("oi%d" % i, [128, 16, 128], BF16) for i in range(2)]
            slb = T4("slb", [128, 384], BF16)
            ub = [T4("ub%d" % i, [128, 1024], BF16) for i in range(3)]
            glb = [T4("glb%d" % i, [128, 1024], BF16) for i in range(2)]
            iota_bf = T4("iota_bf", [128, 128], BF16)
            V(lambda e: e.tensor_copy(out=iota_bf[:], in_=C("iota128")), ["cst"], ["iota_bf"])
            Gst = T4("Gst", [128, 128, 128], BF16)
            S.dma("sync", skl[:], sk_d.rearrange("g k c -> k g c"), writes=["skl"])
            for q4 in range(4):
                bank = "defg"[q4]
                for i in range(4):
                    hp = q4 * 4 + i
                    P(lambda e: e.transpose(out=PS[bank][:, i * 128:(i + 1) * 128], in_=skl[:, hp, :], identity=C("ident")),
                      ["skl", "cst"], ["p" + bank])
                A(lambda e: e.activation(out=skT[:, q4 * 4:(q4 + 1) * 4, :].rearrange("p g t -> p (g t)"), in_=PS[bank][:], func=AF.Copy),
                  ["p" + bank], ["skT"])
            wq_v = wq_d.rearrange("(k p) c -> p k c", p=128)
            for c0 in range(0, 2048, 512):
                S.dma("gpsimd", wq[:, :, c0:c0 + 512], wq_v[:, :, c0:c0 + 512], writes=["wq"])

            def first_stage(i):
                u3 = i % 3
                S.dma("gpsimd", ub[u3][:], pu_d[i], writes=["ub%d" % u3])
                for half in range(2):
                    bank = "c" if half == 0 else "d"
                    for k in range(8):
                        P(lambda e: e.matmul(PS[bank][:], lhsT=ub[u3][:, k * 128:(k + 1) * 128], rhs=h2T[:, half * 4:(half + 1) * 4, k, :],
                                             start=(k == 0), stop=(k == 7)), ["ub%d" % u3], ["p" + bank])
                    A(lambda e: e.activation(out=glb[i % 2][:, half * 512:(half + 1) * 512], in_=PS[bank][:], func=AF.Gelu_apprx_tanh),
                      ["p" + bank], ["glb%d_%d" % (i % 2, half)])
                S.dma("sync", gls_d[i], glb[i % 2][:], reads=["glb%d_0" % (i % 2), "glb%d_1" % (i % 2)], writes=["gls"])

            def stage_a(t):
                hT = h2T[:, t, :, :]
                hk = "h2T%d" % t
                scb = sc2[t % 2]
                for q4 in range(4):
                    bank = "ab"[q4 % 2]
                    for i in range(4):
                        proj_fm(PS[bank][:, i * 128:(i + 1) * 128], "p" + bank, hT, hk, wq, "wq", (q4 * 4 + i) * 128)
                    A(lambda e: e.activation(out=qT_sb[:, q4 * 4:(q4 + 1) * 4, :].rearrange("p g t -> p (g t)"), in_=PS[bank][:],
                                             func=AF.Copy), ["p" + bank], ["qT_sb%d" % q4])
                for q4 in range(4):
                    bank = "ab"[q4 % 2]
                    for i in range(4):
                        hp = q4 * 4 + i
                        P(lambda e: e.matmul(PS[bank][:, i * 128:(i + 1) * 128], lhsT=qT_sb[:, hp, :], rhs=skT[:, hp, :],
                                             start=True, stop=True), ["qT_sb%d" % q4, "skT"], ["p" + bank])
                    A(lambda e: e.activation(out=scb[:, q4 * 4:(q4 + 1) * 4, :].rearrange("p g t -> p (g t)"), in_=PS[bank][:],
                                             func=AF.Copy), ["p" + bank], ["sc%d_%d" % (t % 2, q4)])

            stage_a(0)
            for t in range(8):
                if t + 1 < 8:
                    stage_a(t + 1)
                for i8 in range(4):
                    first_stage(t * 16 + i8)
                sc_sb = sc2[t % 2]
                for hp in range(16):
                    V(lambda e: e.max(out=v16[:, hp, 0:8], in_=sc_sb[:, hp, :]), ["sc%d_%d" % (t % 2, hp // 4)], ["v16a%d" % hp])
                for hp in range(16):
                    V(lambda e: e.max_index(out=i16u[:, hp, 0:8], in_max=v16[:, hp, 0:8], in_values=sc_sb[:, hp, :]),
                      ["sc%d_%d" % (t % 2, hp // 4), "v16a%d" % hp], ["i16ua%d" % hp])
                for hp in range(16):
                    V(lambda e: e.match_replace(out=wk[:, hp, :], in_to_replace=v16[:, hp, 0:8], in_values=sc_sb[:, hp, :], imm_value=NEG),
                      ["sc%d_%d" % (t % 2, hp // 4), "v16a%d" % hp], ["wk%d" % hp])
                for hp in range(16):
                    V(lambda e: e.max(out=v16[:, hp, 8:16], in_=wk[:, hp, :]), ["wk%d" % hp], ["v16b%d" % hp])
                for hp in range(16):
                    V(lambda e: e.max_index(out=i16u[:, hp, 8:16], in_max=v16[:, hp, 8:16], in_values=wk[:, hp, :]), ["wk%d" % hp, "v16b%d" % hp],
                      ["i16ub%d" % hp])
                V(lambda e: e.tensor_copy(out=i16f[:], in_=i16u[:]), ["i16ua%d" % i for i in range(16)] + ["i16ub%d" % i for i in range(16)], ["i16f"])
                v16r = v16[:].rearrange("p (h two) a -> p h two a", two=2)
                i16r = i16f[:].rearrange("p (h two) a -> p h two a", two=2)
                V(lambda e: e.tensor_tensor(out=cand[:], in0=v16r[:, :, 0, :].unsqueeze(3).to_broadcast([128, 8, 16, 16]),
                                            in1=v16r[:, :, 1, :].unsqueeze(2).to_broadcast([128, 8, 16, 16]), op=ALU.add), ["v16a%d" % i for i in range(16)] + ["v16b%d" % i for i in range(16)], ["cand"])
                chs = [cand[:, h, :, :].rearrange("p a b -> p (a b)") for h in range(8)]
                for h in range(8):
                    V(lambda e: e.max(out=tv[:, h, 0:8], in_=chs[h]), ["cand"], ["tva%d" % h])
                for h in range(8):
                    V(lambda e: e.max_index(out=ju[:, h, 0:8], in_max=tv[:, h, 0:8], in_values=chs[h]), ["cand", "tva%d" % h], ["jua%d" % h])
                for h in range(8):
                    V(lambda e: e.match_replace(out=cwk[:, h, :], in_to_replace=tv[:, h, 0:8], in_values=chs[h], imm_value=NEG),
                      ["cand", "tva%d" % h], ["cwk%d" % h])
                for h in range(8):
                    V(lambda e: e.max(out=tv[:, h, 8:16], in_=cwk[:, h, :]), ["cwk%d" % h], ["tvb%d" % h])
                for h in range(8):
                    V(lambda e: e.max_index(out=ju[:, h, 8:16], in_max=tv[:, h, 8:16], in_values=cwk[:, h, :]), ["cwk%d" % h, "tvb%d" % h], ["jub%d" % h])
                V(lambda e: e.tensor_single_scalar(out=au[:], in_=ju[:], scalar=4, op=ALU.logical_shift_right), ["jua%d" % i for i in range(8)] + ["jub%d" % i for i in range(8)], ["au"])
                V(lambda e: e.tensor_single_scalar(out=bu[:], in_=ju[:], scalar=15, op=ALU.bitwise_and), ["jua%d" % i for i in range(8)] + ["jub%d" % i for i in range(8)], ["bu"])
                V(lambda e: e.tensor_copy(out=af[:], in_=au[:]), ["au"], ["af"])
                V(lambda e: e.tensor_copy(out=bf[:], in_=bu[:]), ["bu"], ["bf"])
                io = C("iota16").unsqueeze(1).unsqueeze(1).to_broadcast([128, 8, 16, 16])
                for (src, two, dst, nm) in [(af, 0, i1s, "i1s"), (bf, 1, i2s, "i2s")]:
                    V(lambda e: e.tensor_tensor(out=eq[:], in0=src[:].unsqueeze(3).to_broadcast([128, 8, 16, 16]), in1=io, op=ALU.is_equal),
                      ["af", "bf", "cst"], ["eq"])
                    V(lambda e: e.tensor_tensor(out=eq[:], in0=eq[:], in1=i16r[:, :, two, :].unsqueeze(2).to_broadcast([128, 8, 16, 16]),
                                                op=ALU.mult), ["eq", "i16f"], ["eq"])
                    V(lambda e: e.tensor_reduce(out=dst[:], in_=eq[:], axis=AX.X, op=ALU.add), ["eq"], [nm])
                V(lambda e: e.tensor_tensor(out=ev[:], in0=tv[:], in1=tv[:, :, 0:1].to_broadcast([128, 8, 16]), op=ALU.subtract), ["tva%d" % i for i in range(8)] + ["tvb%d" % i for i in range(8)], ["ev"])
                A(lambda e: e.activation(out=ev[:], in_=ev[:], func=AF.Exp), ["ev"], ["ev"])
                V(lambda e: e.tensor_reduce(out=zs[:], in_=ev[:], axis=AX.X, op=ALU.add), ["ev"], ["zs"])
                V(lambda e: e.reciprocal(out=zs[:], in_=zs[:]), ["zs"], ["zs"])
                V(lambda e: e.tensor_tensor(out=gg[:], in0=ev[:], in1=zs[:].unsqueeze(2).to_broadcast([128, 8, 16]), op=ALU.mult), ["ev", "zs"], ["gg"])
                for n, (src, nm) in enumerate([(i1s, "i1s"), (i2s, "i2s"), (gg, "gg")]):
                    P(lambda e: e.transpose(out=PS["e"][:, n * 128:(n + 1) * 128], in_=src[:].rearrange("p h k -> p (h k)"), identity=C("ident")),
                      [nm, "cst"], ["pe"])
                A(lambda e: e.activation(out=slT[:], in_=PS["e"][:, 0:384], func=AF.Copy), ["pe"], ["slT"])
                V(lambda e: e.tensor_copy(out=slb[:], in_=slT[:]), ["slT"], ["slb"])
                for tb in range(8):
                    ob = tb % 2
                    t0 = tb * 16
                    for fs_i in ([4 + 3 * (tb // 2), 5 + 3 * (tb // 2)] if tb % 2 == 0 else [6 + 3 * (tb // 2)]):
                        first_stage(t * 16 + fs_i)
                    iob = iota_bf[:].unsqueeze(1).to_broadcast([128, 16, 128])
                    V(lambda e: e.tensor_tensor(out=oi[ob][:], in0=iob, in1=slb[:, t0:t0 + 16].unsqueeze(2).to_broadcast([128, 16, 128]),
                                                op=ALU.is_equal), ["iota_bf", "slb"], ["oi%d" % ob])
                    V(lambda e: e.tensor_tensor(out=oj[ob][:], in0=iob, in1=slb[:, 128 + t0:128 + t0 + 16].unsqueeze(2).to_broadcast([128, 16, 128]),
                                                op=ALU.is_equal), ["iota_bf", "slb"], ["oj%d" % ob])
                    V(lambda e: e.tensor_tensor(out=oj[ob][:], in0=oj[ob][:], in1=slb[:, 256 + t0:256 + t0 + 16].unsqueeze(2).to_broadcast([128, 16, 128]),
                                                op=ALU.mult), ["oj%d" % ob, "slb"], ["oj%d" % ob])
                    for q in range(16):
                        tok = t0 + q
                        r4 = tok % 4
                        bank = "f" if (tok // 4) % 2 == 0 else "g"
                        P(lambda e: e.matmul(PS[bank][:, r4 * 128:(r4 + 1) * 128], lhsT=oj[ob][:, q, :], rhs=oi[ob][:, q, :], start=True, stop=True),
                          ["oj%d" % ob, "oi%d" % ob], ["p" + bank])
                        if r4 == 3:
                            A(lambda e: e.activation(out=Gst[:, :, tok - 3:tok + 1], in_=PS[bank][:].rearrange("p (q i) -> p i q", q=4), func=AF.Copy),
                              ["p" + bank], ["Gst"])
                for i0 in range(0, 128, 16):
                    S.dma("sync", gd_d[i0:i0 + 16, :, t * 128:(t + 1) * 128].rearrange("i j t -> j i t"), Gst[:, i0:i0 + 16, :],
                          reads=["Gst"], writes=["gd"])
            S.barrier()

        with ExitStack() as st5:
            T5 = mk(st5)
            acc = T5("acc", [128, 8, 1024])
            Ag = [T5("Ag%d" % i, [128, 8, 1024], BF16) for i in range(2)]
            vg = [T5("vg%d" % i, [128, 8, 1024], BF16) for i in range(2)]
            glr = [T5("glr%d" % i, [128, 1024], BF16) for i in range(3)]
            gt = [T5("gt%d" % i, [128, 1024], BF16) for i in range(3)]
            gate2b = T5("gate2b", [128, 2, 1024]); fgb = T5("fgb", [128, 1024])
            x1r = T5("x1r", [128, 1024]); x2 = T5("x2", [128, 1024]); yo = T5("yo", [128, 1024])
            S.dma("sync", fgb[:], fg_d.partition_broadcast(128), writes=["fgb"])
            for j in range(2):
                bcast_tile(gate2b[:, j, :], modT[:, 40:48, j], ["modT"], "gate2b%d" % j)
            def prep_one(grp, e8):
                gp = grp % 2
                if True:
                    i = grp * 8 + e8
                    u3 = i % 3
                    S.dma("sync", glr[u3][:], gls_d[i], writes=["glr%d" % u3])
                    S.dma("gpsimd", vg[gp][:, e8, :], pv_d[i * 128:(i + 1) * 128, :], writes=["vg%d_%d" % (gp, e8)])
                    S.dma("sync", gt[u3][:], gd_d[i], writes=["gt%d" % u3])
                    V(lambda e: e.tensor_tensor(out=Ag[gp][:, e8, :], in0=glr[u3][:], in1=gt[u3][:], op=ALU.mult),
                      ["glr%d" % u3, "gt%d" % u3], ["Ag%d_%d_0" % (gp, e8), "Ag%d_%d_1" % (gp, e8)])

            def second_stage(grp):
                gp = grp % 2
                for tile in range(8):
                    if grp + 1 < 16:
                        prep_one(grp + 1, tile)
                    banks = ("d", "e") if tile % 2 == 0 else ("f", "g")
                    for e8 in range(8):
                        for half in range(2):
                            P(lambda e: e.matmul(PS[banks[half]][:], lhsT=Ag[gp][:, e8, tile * 128:(tile + 1) * 128],
                                                 rhs=vg[gp][:, e8, half * 512:(half + 1) * 512], start=(e8 == 0), stop=(e8 == 7)),
                              ["Ag%d_%d_%d" % (gp, e8, tile // 4), "vg%d_%d" % (gp, e8)], ["p" + banks[half]])
                    for half in range(2):
                        dst = acc[:, tile, half * 512:(half + 1) * 512]
                        akey = "acc%d_%d" % (tile, half)
                        if grp == 0:
                            V(lambda e: e.tensor_copy(out=dst, in_=PS[banks[half]][:]), ["p" + banks[half]], [akey])
                        else:
                            V(lambda e: e.tensor_tensor(out=dst, in0=PS[banks[half]][:], in1=dst, op=ALU.add), ["p" + banks[half], akey], [akey])

            for e8_ in range(8):
                prep_one(0, e8_)
            for grp in range(16):
                second_stage(grp)
            for t in range(8):
                j = 0 if t < 4 else 1
                S.dma("sync", x1r[:], x1s_d[t * 128:(t + 1) * 128, :], writes=["x1r"])
                V(lambda e: e.tensor_tensor(out=x2[:], in0=acc[:, t, :], in1=gate2b[:, j, :], op=ALU.mult), ["acc%d_0" % t, "acc%d_1" % t], ["x2"])
                V(lambda e: e.tensor_tensor(out=x2[:], in0=x2[:], in1=x1r[:], op=ALU.add), ["x2", "x1r"], ["x2"])
                A(lambda e: e.activation(out=junkb[:], in_=x2[:], func=AF.Square, accum_out=stat[:, 0:1]), ["x2"], ["junkb", "stat0"])
                A(lambda e: e.activation(out=stat[:, 1:2], in_=stat[:, 0:1], func=AF.Ln, scale=1.0 / 1024.0, bias=EPS), ["stat0"], ["stat1"])
                A(lambda e: e.activation(out=stat[:, 2:3], in_=stat[:, 1:2], func=AF.Exp, scale=-0.5), ["stat1"], ["stat2"])
                V(lambda e: e.scalar_tensor_tensor(out=yo[:], in0=x2[:], scalar=stat[:, 2:3], in1=fgb[:], op0=ALU.mult, op1=ALU.mult),
                  ["x2", "stat2", "fgb"], ["yo"])
                dsto = yp_d[t * 128:(t + 1) * 128, :] if t < 4 else ys_d[(t - 4) * 128:(t - 3) * 128, :]
                S.dma("sync", dsto, yo[:], reads=["yo"])
            S.finish()
            S.emit()
    return nc


_PROGRAM = {}


def _prep_inputs(inp):
    f = lambda a: np.ascontiguousarray(np.asarray(a, dtype=np.float32))
    x_prompt = f(inp["x_prompt"]); x_sample = f(inp["x_sample"]); state = f(inp["state_hgrn"])
    c = f(inp["c"]); c_ctx = f(inp["c_ctx"])
    w_in = f(inp["w_in"])[0]
    hgrn_lb = f(inp["hgrn_lb"])
    shared = {
        "cst": _CST,
        "w_ada": f(inp["w_ada"])[0],
        "b_adaT": f(f(inp["b_ada"])[0].reshape(48, 128).T),
        "n1T": f(f(inp["norm1_g"])[0].reshape(8, 128).T),
        "n2T": f(f(inp["norm2_g"])[0].reshape(8, 128).T),
        "w_in": w_in,
        "sguT": f(f(inp["sgu_norm_g"])[0].reshape(4, 128).T),
        "w_sp": f(inp["w_spatial"])[0],
        "b_sp": f(f(inp["b_spatial"])[0].reshape(1, 512)),
        "lb": f(hgrn_lb.reshape(1, 2048)),
        "hgT": f(f(inp["hgrn_norm_g"])[0].T),
        "wpa": f(inp["w_proj_a"])[0], "wpb": f(inp["w_proj_b"])[0], "wo": f(inp["w_out"])[0],
        "wq": f(inp["peer_w_q"])[0],
        "sk": f(f(inp["peer_sub_keys"])[0].reshape(16, 128, 128)),
        "pu": f(f(inp["peer_u"])[0].reshape(128, 128, 8, 128).transpose(0, 3, 2, 1).reshape(128, 128, 1024)),
        "pv": f(inp["peer_v"])[0],
        "fg": f(f(inp["final_norm_g"]).reshape(1, 1024)),
    }
    wf = [w_in[:, 1536:2048], w_in[:, 2048:2560]]
    maps = []
    for core in range(8):
        b, j = core // 4, core % 4
        m = dict(shared)
        m["xp"] = f(x_prompt[2 * core:2 * core + 2].reshape(512, 1024))
        m["xs"] = f(x_sample[b, 512 * j:512 * (j + 1)])
        slots = [(s, 0) for s in range(0, j)] + [(s, 1) for s in range(3, j, -1)]
        xo = np.zeros((3, 512, 1024), np.float32)
        tok_idx = np.zeros((16, 128), np.int64)
        for tt in range(4):
            tok_idx[tt] = 512 * j + tt * 128 + np.arange(128)
        wfs = np.zeros((3, 1024, 512), np.float32)
        lbs = np.zeros((3, 2, 512), np.float32)
        alpha = np.zeros((128, 3), np.float32)
        for si, (seg, d) in enumerate(slots):
            toks = 512 * seg + np.arange(512)
            if d == 1:
                toks = toks[::-1]
            xo[si] = x_sample[b, toks]
            for tt in range(4):
                tok_idx[4 + si * 4 + tt] = toks[tt * 128:(tt + 1) * 128]
            wfs[si] = wf[d]
            lbs[si] = hgrn_lb[:, d, :]
            alpha[:, si] = 1.0 if d == 0 else 0.0
        sel = np.zeros((16, 2, 128, 128), np.float32)
        ar = np.arange(128)
        for ti in range(16):
            sel[ti, 0, tok_idx[ti] // 64, ar] = 1.0
            sel[ti, 1, tok_idx[ti] % 64, ar] = 1.0
        m["xo"] = xo; m["sel"] = sel; m["wfs"] = wfs; m["lbs"] = f(lbs.reshape(1, 3072)); m["alpha"] = alpha
        m["s0"] = f(state[b, 0])
        cond2 = np.stack([c_ctx, c[b]], axis=0)
        m["condT"] = f(cond2.reshape(2, 8, 128).transpose(2, 1, 0).reshape(128, 16))
        maps.append(m)
    return maps


def run(inp, debug=False):
    key = bool(debug)
    if key not in _PROGRAM:
        _PROGRAM[key] = build_program(debug)
    nc = _PROGRAM[key]
    maps = _prep_inputs(inp)
    res = run_bass_kernel_spmd(nc, maps, core_ids=list(range(8)))
    return res.results


def kernel(**inputs):
    rs = run(inputs, debug=False)
    y_prompt = np.concatenate([r["yp"].reshape(2, 256, 1024) for r in rs], axis=0).astype(np.float32)
    y_sample = np.stack([np.concatenate([rs[b * 4 + j]["ys"] for j in range(4)], axis=0) for b in range(2)], axis=0).astype(np.float32)
    st = np.concatenate([r["st"].reshape(2, 1, 2, 4, 128, 128) for r in rs], axis=0).astype(np.float32)
    return (y_prompt, y_sample, st)
```
